# Optimizing a Trainium2 kernel written in Bass

```python
import math
import jax, jax.numpy as jnp
from jax import lax
import numpy as np

D_MODEL = 2048
BATCH = 4
SEQ = 4096
DEPTH = 4

GRID_W = 64
CTX_LEN = 256
HEAD_DIM = 128
W_GDN = D_MODEL // 4
W_GQA = D_MODEL // 2
W_LRU = D_MODEL - W_GDN - W_GQA
GDN_HEADS = W_GDN // HEAD_DIM
GDN_CHUNK = 64
GDN_CONV = 4
GQA_Q_HEADS = W_GQA // HEAD_DIM
GQA_KV_HEADS = GQA_Q_HEADS // 4
Q_BLOCK = 128
ROPE_THETA = 10000.0
LRU_BLOCKS = 8
LRU_BLOCK = W_LRU // LRU_BLOCKS
LRU_CONV = 4
LRU_C = 8.0
FFN_DIM = ((8 * D_MODEL // 3 + 127) // 128) * 128
FFN_CONV = 3
N_MOD = 6
EPS = 1e-6
IN_SPLITS = (3 * W_GDN, W_GDN, GDN_HEADS, GDN_HEADS, GDN_HEADS, GDN_HEADS,
             W_GQA, GQA_KV_HEADS * HEAD_DIM, GQA_KV_HEADS * HEAD_DIM, W_LRU, W_LRU)
IN_COLS = 4 * W_GDN + 4 * GDN_HEADS + W_GQA + 2 * GQA_KV_HEADS * HEAD_DIM + 2 * W_LRU

kernel_name = 'hybrid_parallel_heads_flow_block'


def rmsnorm(x, g):
    xf = x.astype(jnp.float32)
    y = xf * lax.rsqrt(jnp.mean(xf * xf, axis=-1, keepdims=True) + EPS)
    return y.astype(x.dtype) * g


def l2norm(x):
    xf = x.astype(jnp.float32)
    return (xf * lax.rsqrt(jnp.sum(xf * xf, axis=-1, keepdims=True) + EPS)).astype(x.dtype)


def modulate(h, shift, scale):
    return h * (1 + scale) + shift


def adaln(cond, w, b):
    return jax.nn.silu(cond) @ w + b


def depthwise_conv(x, w, pad_left):
    width, t = w.shape[0], x.shape[1]
    xp = jnp.pad(x, ((0, 0), (pad_left, width - 1 - pad_left), (0, 0)))
    y = xp[:, 0:t] * w[0]
    for j in range(1, width):
        y = y + xp[:, j:j + t] * w[j]
    return y


def identity(t):
    return t


def reverse(t):
    return jnp.flip(t, axis=1)


def split_columns(p):
    bounds, acc = [], 0
    for s in IN_SPLITS[:-1]:
        acc += s
        bounds.append(acc)
    return jnp.split(p, bounds, axis=-1)


def axial_rope_tables(n_tokens):
    rows = n_tokens // GRID_W
    row = jnp.repeat(jnp.arange(rows, dtype=jnp.float32), GRID_W)
    col = jnp.tile(jnp.arange(GRID_W, dtype=jnp.float32), rows)
    axis_dim = HEAD_DIM // 2
    inv_freq = ROPE_THETA ** (-jnp.arange(0, axis_dim, 2, dtype=jnp.float32) / axis_dim)
    ang = jnp.concatenate([row[:, None] * inv_freq, col[:, None] * inv_freq], axis=-1)
    return jnp.cos(ang), jnp.sin(ang)


def apply_axial_rope(x, cos, sin):
    b, t, h, d = x.shape
    xr = x.reshape(b, t, h, 2, 2, d // 4)
    x1, x2 = xr[..., 0, :], xr[..., 1, :]
    c = cos.reshape(t, 1, 2, d // 4)
    s = sin.reshape(t, 1, 2, d // 4)
    out = jnp.stack([x1 * c - x2 * s, x2 * c + x1 * s], axis=-2)
    return out.reshape(b, t, h, d).astype(x.dtype)


def gated_delta_chunked(q, k, v, log_a, beta, s0):
    b, t, h, dk = q.shape
    dv = v.shape[-1]
    n = t // GDN_CHUNK

    def chunks(z):
        z = z.astype(jnp.float32).reshape(b, n, GDN_CHUNK, h, *z.shape[3:])
        return jnp.moveaxis(z, (1, 3), (0, 2))

    qc, kc, vc = chunks(q), chunks(k), chunks(v)
    la, bt = chunks(log_a), chunks(beta)
    g = jnp.cumsum(la, axis=-1)
    idx = jnp.arange(GDN_CHUNK)
    incl = idx[:, None] >= idx[None, :]
    strict = idx[:, None] > idx[None, :]
    dmask = jnp.exp(jnp.where(incl, g[..., :, None] - g[..., None, :], -jnp.inf))
    kb = kc * bt[..., None]
    lower = jnp.einsum('nbhcd,nbhsd->nbhcs', kb, kc) * jnp.where(strict, dmask, 0.0)
    lhs = lower + jnp.eye(GDN_CHUNK, dtype=jnp.float32)
    rhs = jnp.concatenate([vc * bt[..., None], kb * jnp.exp(g)[..., None]], axis=-1)
    sol = lax.linalg.triangular_solve(lhs, rhs, left_side=True, lower=True, unit_diagonal=True)
    u, w = sol[..., :dv], sol[..., dv:]
    attn = jnp.einsum('nbhcd,nbhsd->nbhcs', qc, kc) * dmask
    q_dec = qc * jnp.exp(g)[..., None]
    g_last = g[..., -1]
    k_dec = kc * jnp.exp(g_last[..., None] - g)[..., None]

    def step(s, xs):
        u_i, w_i, q_i, a_i, k_i, gl_i = xs
        v_new = u_i - jnp.einsum('bhcd,bhde->bhce', w_i, s)
        o_i = jnp.einsum('bhcd,bhde->bhce', q_i, s) + jnp.einsum('bhcs,bhse->bhce', a_i, v_new)
        s = s * jnp.exp(gl_i)[..., None, None] + jnp.einsum('bhcd,bhce->bhde', k_i, v_new)
        return s, o_i

    s_final, o = lax.scan(step, s0.astype(jnp.float32), (u, w, q_dec, attn, k_dec, g_last))
    o = jnp.moveaxis(o, (0, 2), (1, 3)).reshape(b, t, h, dv)
    return o.astype(v.dtype), s_final


def gdn_stream(parts, conv_w, a_log, dt_bias):
    qkv, z, b_f, b_b, a_f, a_b = parts
    b, t, _ = qkv.shape
    qkv = jax.nn.silu(depthwise_conv(qkv, conv_w, GDN_CONV // 2))
    q, k, v = jnp.split(qkv, 3, axis=-1)
    q = l2norm(q.reshape(b, t, GDN_HEADS, HEAD_DIM)) * (HEAD_DIM ** -0.5)
    k = l2norm(k.reshape(b, t, GDN_HEADS, HEAD_DIM))
    v = v.reshape(b, t, GDN_HEADS, HEAD_DIM)
    a_log = a_log.astype(jnp.float32)
    dt_bias = dt_bias.astype(jnp.float32)
    dirs = []
    for d, (a_raw, b_raw) in enumerate(((a_f, b_f), (a_b, b_b))):
        log_a = -jnp.exp(a_log[d]) * jax.nn.softplus(a_raw.astype(jnp.float32) + dt_bias[d])
        dirs.append((log_a, jax.nn.sigmoid(b_raw.astype(jnp.float32))))
    return q, k, v, z, dirs


def gated_rmsnorm(o, z, g):
    b, t, h, dv = o.shape
    y = rmsnorm(o, g) * jax.nn.silu(z.reshape(b, t, h, dv))
    return y.reshape(b, t, h * dv)


def gdn_mixer(parts_l, parts_c, conv_w, a_log, dt_bias, norm_g, ctx_out):
    ql, kl, vl, zl, dl = gdn_stream(parts_l, conv_w, a_log, dt_bias)
    qc, kc, vc, zc, dc = gdn_stream(parts_c, conv_w, a_log, dt_bias)
    s0 = jnp.zeros((qc.shape[0], GDN_HEADS, HEAD_DIM, HEAD_DIM), jnp.float32)
    outs_l, outs_c = [], []
    for d, f in enumerate((identity, reverse)):
        (la_c, bt_c), (la_l, bt_l) = dc[d], dl[d]
        oc, sc = gated_delta_chunked(f(qc), f(kc), f(vc), f(la_c), f(bt_c), s0)
        ol, _ = gated_delta_chunked(f(ql), f(kl), f(vl), f(la_l), f(bt_l), sc)
        outs_l.append(f(ol))
        outs_c.append(f(oc))
    y_l = gated_rmsnorm(outs_l[0] + outs_l[1], zl, norm_g)
    y_c = gated_rmsnorm(outs_c[0] + outs_c[1], zc, norm_g) if ctx_out else None
    return y_l, y_c


def block_attention(q, k, v):
    b, t, hq, d = q.shape
    hkv = k.shape[2]
    grp = hq // hkv
    nb = t // Q_BLOCK
    qb = q.reshape(b, nb, Q_BLOCK, hkv, grp, d).swapaxes(0, 1)
    scale = d ** -0.5

    def one_block(qi):
        s = jnp.einsum('bqkgd,bskd->bkgqs', qi, k).astype(jnp.float32) * scale
        p = jax.nn.softmax(s, axis=-1).astype(v.dtype)
        return jnp.einsum('bkgqs,bskd->bqkgd', p, v)

    o = lax.map(one_block, qb)
    return o.swapaxes(0, 1).reshape(b, t, hq * d)


def gqa_mixer(parts_l, parts_c, q_norm_g, k_norm_g, cos, sin, ctx_out):
    def heads(q, k, v):
        b, t, _ = q.shape
        return (rmsnorm(q.reshape(b, t, GQA_Q_HEADS, HEAD_DIM), q_norm_g),
                rmsnorm(k.reshape(b, t, GQA_KV_HEADS, HEAD_DIM), k_norm_g),
                v.reshape(b, t, GQA_KV_HEADS, HEAD_DIM))

    ql, kl, vl = heads(*parts_l)
    qc, kc, vc = heads(*parts_c)
    ql = apply_axial_rope(ql, cos, sin)
    kl = apply_axial_rope(kl, cos, sin)
    y_l = block_attention(ql, jnp.concatenate([kl, kc], axis=1), jnp.concatenate([vl, vc], axis=1))
    y_c = block_attention(qc, kc, vc) if ctx_out else None
    return y_l, y_c


def linear_combine(left, right):
    a1, b1 = left
    a2, b2 = right
    return a1 * a2, a2 * b1 + b2


def rglru(x, gate_w, gate_b, lam, h0):
    b, t, _ = x.shape
    xb = x.reshape(b, t, LRU_BLOCKS, LRU_BLOCK)
    gates = jnp.einsum('btnd,gnde->gbtne', xb, gate_w).reshape(2, b, t, W_LRU) + gate_b[:, None, None, :]
    gates = jax.nn.sigmoid(gates.astype(jnp.float32))
    r, i = gates[0], gates[1]
    log_a = -LRU_C * r * jax.nn.softplus(-lam.astype(jnp.float32))
    a = jnp.exp(log_a)
    u = jnp.sqrt(-jnp.expm1(2.0 * log_a)) * (i * x.astype(jnp.float32))
    u = u.at[:, 0].add(a[:, 0] * h0)
    _, h = lax.associative_scan(linear_combine, (a, u), axis=1)
    return h.astype(x.dtype), h[:, -1]


def lru_mixer(parts_l, parts_c, conv_w, conv_b, gate_w, gate_b, lam, ctx_out):
    xl, gl = parts_l
    xc, gc = parts_c
    xl = depthwise_conv(xl, conv_w, LRU_CONV // 2) + conv_b
    xc = depthwise_conv(xc, conv_w, LRU_CONV // 2) + conv_b
    h0 = jnp.zeros((xc.shape[0], W_LRU), jnp.float32)
    outs_l, outs_c = [], []
    for d, f in enumerate((identity, reverse)):
        hc, hc_last = rglru(f(xc), gate_w[d], gate_b[d], lam[d], h0)
        hl, _ = rglru(f(xl), gate_w[d], gate_b[d], lam[d], hc_last)
        outs_l.append(f(hl))
        outs_c.append(f(hc))
    y_l = jax.nn.gelu(gl) * (outs_l[0] + outs_l[1])
    y_c = jax.nn.gelu(gc) * (outs_c[0] + outs_c[1]) if ctx_out else None
    return y_l, y_c


def conv_ffn(h, w_up, conv_w, conv_b, w_down):
    u = depthwise_conv(h @ w_up, conv_w, FFN_CONV // 2) + conv_b
    gate, up = jnp.split(u, 2, axis=-1)
    return (jax.nn.silu(gate) * up) @ w_down


def trunk_layer(x, ctx, mod_l, mod_c, cos, sin, norm1_g, norm2_g, w_in,
                gdn_conv_w, gdn_a_log, gdn_dt_bias, gdn_norm_g, q_norm_g, k_norm_g,
                lru_conv_w, lru_conv_b, lru_gate_w, lru_gate_b, lru_lambda,
                w_out, ffn_w_up, ffn_conv_w, ffn_conv_b, ffn_w_down, ctx_out):
    sh1_l, sc1_l, g1_l, sh2_l, sc2_l, g2_l = jnp.split(mod_l, N_MOD, axis=-1)
    sh1_c, sc1_c, g1_c, sh2_c, sc2_c, g2_c = jnp.split(mod_c, N_MOD, axis=-1)
    h_l = modulate(rmsnorm(x, norm1_g), sh1_l, sc1_l)
    h_c = modulate(rmsnorm(ctx, norm1_g), sh1_c, sc1_c)
    p_l = split_columns(h_l @ w_in)
    p_c = split_columns(h_c @ w_in)
    ya_l, ya_c = gdn_mixer(p_l[0:6], p_c[0:6], gdn_conv_w, gdn_a_log, gdn_dt_bias, gdn_norm_g, ctx_out)
    yb_l, yb_c = gqa_mixer(p_l[6:9], p_c[6:9], q_norm_g, k_norm_g, cos, sin, ctx_out)
    yc_l, yc_c = lru_mixer(p_l[9:11], p_c[9:11], lru_conv_w, lru_conv_b, lru_gate_w, lru_gate_b, lru_lambda, ctx_out)
    x = x + g1_l * (jnp.concatenate([ya_l, yb_l, yc_l], axis=-1) @ w_out)
    x = x + g2_l * conv_ffn(modulate(rmsnorm(x, norm2_g), sh2_l, sc2_l), ffn_w_up, ffn_conv_w, ffn_conv_b, ffn_w_down)
    if ctx_out:
        ctx = ctx + g1_c * (jnp.concatenate([ya_c, yb_c, yc_c], axis=-1) @ w_out)
        ctx = ctx + g2_c * conv_ffn(modulate(rmsnorm(ctx, norm2_g), sh2_c, sc2_c), ffn_w_up, ffn_conv_w, ffn_conv_b, ffn_w_down)
    return x, ctx


def setup_inputs(seed: int = 0) -> dict:
    key = jax.random.key(seed)
    ks = jax.random.split(key, 32)
    f32 = jnp.float32

    def nrm(k, shape, scale):
        return jax.random.normal(k, shape, f32) * scale

    def gain(k, shape):
        return 1.0 + 0.02 * jax.random.normal(k, shape, f32)

    a_init = jax.random.uniform(ks[8], (DEPTH, 2, GDN_HEADS), f32, 1.0, 16.0)
    dt = jnp.exp(jax.random.uniform(ks[9], (DEPTH, 2, GDN_HEADS), f32, math.log(1e-3), math.log(1e-1)))
    a_lru = jax.random.uniform(ks[17], (DEPTH, 2, W_LRU), f32, 0.9, 0.999) ** (1.0 / LRU_C)
    return {
        'x': nrm(ks[0], (BATCH, SEQ, D_MODEL), 1.0),
        'c': nrm(ks[1], (BATCH, D_MODEL), 1.0),
        'ctx': nrm(ks[2], (BATCH, CTX_LEN, D_MODEL), 1.0),
        'c_ctx': nrm(ks[3], (D_MODEL,), 1.0),
        'ada_w': nrm(ks[4], (DEPTH, D_MODEL, N_MOD * D_MODEL), 0.5 * D_MODEL ** -0.5),
        'ada_b': nrm(ks[5], (DEPTH, N_MOD * D_MODEL), 0.02),
        'norm1_g': gain(ks[6], (DEPTH, D_MODEL)),
        'norm2_g': gain(ks[7], (DEPTH, D_MODEL)),
        'w_in': nrm(ks[10], (DEPTH, D_MODEL, IN_COLS), D_MODEL ** -0.5),
        'gdn_conv_w': nrm(ks[11], (DEPTH, GDN_CONV, 3 * W_GDN), GDN_CONV ** -0.5),
        'gdn_a_log': jnp.log(a_init),
        'gdn_dt_bias': dt + jnp.log(-jnp.expm1(-dt)),
        'gdn_norm_g': gain(ks[12], (DEPTH, HEAD_DIM)),
        'q_norm_g': gain(ks[13], (DEPTH, HEAD_DIM)),
        'k_norm_g': gain(ks[14], (DEPTH, HEAD_DIM)),
        'lru_conv_w': nrm(ks[15], (DEPTH, LRU_CONV, W_LRU), LRU_CONV ** -0.5),
        'lru_conv_b': nrm(ks[16], (DEPTH, W_LRU), 0.02),
        'lru_gate_w': nrm(ks[18], (DEPTH, 2, 2, LRU_BLOCKS, LRU_BLOCK, LRU_BLOCK), LRU_BLOCK ** -0.5),
        'lru_gate_b': nrm(ks[19], (DEPTH, 2, 2, W_LRU), 0.02),
        'lru_lambda': jnp.log(a_lru) - jnp.log1p(-a_lru),
        'w_out': nrm(ks[20], (DEPTH, D_MODEL, D_MODEL), D_MODEL ** -0.5),
        'ffn_w_up': nrm(ks[21], (DEPTH, D_MODEL, 2 * FFN_DIM), D_MODEL ** -0.5),
        'ffn_conv_w': nrm(ks[22], (DEPTH, FFN_CONV, 2 * FFN_DIM), FFN_CONV ** -0.5),
        'ffn_conv_b': nrm(ks[23], (DEPTH, 2 * FFN_DIM), 0.02),
        'ffn_w_down': nrm(ks[24], (DEPTH, FFN_DIM, D_MODEL), FFN_DIM ** -0.5),
        'final_norm_g': gain(ks[25], (D_MODEL,)),
    }


def reference(x, c, ctx, c_ctx, ada_w, ada_b, norm1_g, norm2_g, w_in, gdn_conv_w, gdn_a_log, gdn_dt_bias,
              gdn_norm_g, q_norm_g, k_norm_g, lru_conv_w, lru_conv_b, lru_gate_w, lru_gate_b, lru_lambda,
              w_out, ffn_w_up, ffn_conv_w, ffn_conv_b, ffn_w_down, final_norm_g):
    cos, sin = axial_rope_tables(x.shape[1])
    for layer in range(DEPTH):
        mod_l = adaln(c, ada_w[layer], ada_b[layer])[:, None, :]
        mod_c = adaln(c_ctx, ada_w[layer], ada_b[layer])[None, None, :]
        x, ctx = trunk_layer(
            x, ctx, mod_l, mod_c, cos, sin, norm1_g[layer], norm2_g[layer], w_in[layer],
            gdn_conv_w[layer], gdn_a_log[layer], gdn_dt_bias[layer], gdn_norm_g[layer],
            q_norm_g[layer], k_norm_g[layer], lru_conv_w[layer], lru_conv_b[layer],
            lru_gate_w[layer], lru_gate_b[layer], lru_lambda[layer], w_out[layer],
            ffn_w_up[layer], ffn_conv_w[layer], ffn_conv_b[layer], ffn_w_down[layer],
            layer < DEPTH - 1)
    return rmsnorm(x, final_norm_g)
```

```python
import math
from contextlib import ExitStack
import numpy as np
import concourse.bass as bass
import concourse.mybir as mybir
from concourse.bass_utils import run_bass_kernel_spmd

F32 = mybir.dt.float32
BF16 = mybir.dt.bfloat16
AF = mybir.ActivationFunctionType
ALU = mybir.AluOpType
AX = mybir.AxisListType

D = 2048
DEPTH = 4
NLAT = 4096
NCTX = 256
TOK = NLAT + NCTX
KC = D // 128
HD = 128
FFN = 5504
INC = 4624
EPS = 1e-6
TILES = [(i * 512, 512) for i in range(8)] + [(NLAT, NCTX)]
UW = TOK + 8
NCH = TOK // 128
BIG = 80.0
SELF_WAITS = True
ALL_INC = True
import os as _os
GX = int(_os.environ.get('GX', '0'))


def ucol(t):
    return t + 2 if t < NLAT else t + 6


class Prog:
    def __init__(self, n_dma_sems=16):
        nc = bass.Bass("TRN2", target_bir_lowering=False)
        self.nc = nc
        self.eng = dict(pe=nc.tensor, act=nc.scalar, dve=nc.vector, pool=nc.gpsimd, sp=nc.sync)
        self.sem = {}
        self.cnt = {}
        for e in ("pe", "act", "dve", "pool"):
            self.sem[e] = nc.alloc_semaphore("s_" + e)
            self.cnt[e] = 0
        self.dq = []
        for i in range(n_dma_sems):
            nm = "d%d" % i
            self.sem[nm] = nc.alloc_semaphore("s_" + nm)
            self.cnt[nm] = 0
            self.dq.append(nm)
        self.dq_rr = 0
        self.waited = {e: {} for e in self.eng}
        self.lastw = {}
        self.reads = {}
        self.n_inst = 0

    def _need(self, r, w):
        need = {}
        for k in r:
            lw = self.lastw.get(k)
            if lw is not None:
                need[lw[0]] = max(need.get(lw[0], 0), lw[1])
        for k in w:
            lw = self.lastw.get(k)
            if lw is not None:
                need[lw[0]] = max(need.get(lw[0], 0), lw[1])
            for e, i in self.reads.get(k, {}).items():
                need[e] = max(need.get(e, 0), i)
        return need

    def _waits(self, issuer, need, skip_self=False):
        wd = self.waited[issuer]
        for s, v in need.items():
            if skip_self and s == issuer:
                continue
            if wd.get(s, 0) >= v:
                continue
            self.eng[issuer].wait_ge(self.sem[s], v)
            wd[s] = v

    def _mark(self, who, idx, r, w):
        for k in w:
            self.lastw[k] = (who, idx)
            self.reads[k] = {}
        for k in r:
            d = self.reads.setdefault(k, {})
            d[who] = max(d.get(who, 0), idx)

    def op(self, e, fn, r=(), w=()):
        need = self._need(r, w)
        self._waits(e, need, skip_self=(e == "pe"))
        ins = fn(self.eng[e])
        ins.then_inc(self.sem[e], 1)
        self.cnt[e] += 1
        self._mark(e, self.cnt[e], r, w)
        self.n_inst += 1

    def mm(self, fns, r=(), w=()):
        need = self._need(r, w)
        self._waits("pe", need, skip_self=True)
        ins = None
        for fn in fns:
            ins = fn(self.eng["pe"])
            self.n_inst += 1
        ins.then_inc(self.sem["pe"], 1)
        self.cnt["pe"] += 1
        self._mark("pe", self.cnt["pe"], r, w)

    def dma(self, issuer, out, in_, r=(), w=()):
        need = self._need(r, w)
        q = self.dq[self.dq_rr]
        self.dq_rr = (self.dq_rr + 1) % len(self.dq)
        if self.cnt[q] > 0:
            need[q] = max(need.get(q, 0), self.cnt[q])
        self._waits(issuer, need)
        ins = self.eng[issuer].dma_start(out=out, in_=in_)
        ins.then_inc(self.sem[q], 16)
        self.cnt[q] += 16
        self._mark(q, self.cnt[q], r, w)
        self.n_inst += 1

    def barrier(self):
        need = {s: v for s, v in self.cnt.items() if v > 0}
        for e in self.eng:
            self._waits(e, dict(need))
        self.lastw.clear()
        self.reads.clear()

    def emit(self):
        pass


class Buf:
    _n = 0

    def __init__(self, t, name=None):
        self.t = t
        Buf._n += 1
        self.id = name or ("b%d" % Buf._n)

    def __getitem__(self, idx):
        return self.t[idx]

    def k(self, *slot):
        return (self.id,) + slot


class Ring:
    def __init__(self, bufs):
        self.b = bufs
        self.i = 0

    def get(self):
        b = self.b[self.i % len(self.b)]
        self.i += 1
        return b


def _consts():
    i = np.arange(128)
    c = {}
    c["ident"] = np.eye(128, dtype=np.float32)
    c["ones"] = np.ones((128, 128), np.float32)
    perm = np.zeros((128, 128), np.float32)
    for d in range(128):
        j = d % 64
        partner = d + 32 if j < 32 else d - 32
        perm[partner, d] = 1.0
    c["perm"] = perm
    incF = (i[:, None] >= i[None, :]).astype(np.float32)
    incB = (i[:, None] <= i[None, :]).astype(np.float32)
    strF = (i[:, None] > i[None, :]).astype(np.float32)
    strB = (i[:, None] < i[None, :]).astype(np.float32)
    c["inctF"] = incF.T.copy(); c["inctB"] = incB.T.copy()
    c["strF"] = strF; c["strB"] = strB
    c["negF"] = (1.0 - incF) * BIG; c["negB"] = (1.0 - incB) * BIG
    c["negtF"] = -c["negF"].T.copy(); c["negtB"] = -c["negB"].T.copy()
    c["offd"] = 1.0 - np.eye(128, dtype=np.float32)
    bd = np.zeros((128, 128), np.float32); bd[:64, :64] = 1.0; bd[64:, 64:] = 1.0
    c["bd"] = bd; c["offb"] = 1.0 - bd
    names = list(c.keys())
    arr = np.concatenate([c[n] for n in names], axis=1)
    off = {n: k * 128 for k, n in enumerate(names)}
    return arr, off


def _rope_tables():
    rows = NLAT // 64
    row = np.repeat(np.arange(rows, dtype=np.float32), 64)
    col = np.tile(np.arange(64, dtype=np.float32), rows)
    inv_freq = (10000.0 ** (-(np.arange(0, 64, 2, dtype=np.float32)) / np.float32(64))).astype(np.float32)
    cosT = np.zeros((128, NLAT), np.float32)
    sinT = np.zeros((128, NLAT), np.float32)
    for d in range(128):
        axis = d // 64
        j = d % 64
        half = j // 32
        f = j % 32
        pos = row if axis == 0 else col
        ang = (pos * inv_freq[f]).astype(np.float32)
        cosT[d] = np.cos(ang)
        sinT[d] = np.sin(ang) * (-1.0 if half == 0 else 1.0)
    return np.stack([cosT, sinT])


def _prm_layout():
    off = {}
    o = 0
    for l in range(DEPTH):
        for name, n in (("ada_b", 96), ("n1g", 16), ("n2g", 16), ("gconv", 48), ("gng", 1), ("qng", 1), ("kng", 1),
                        ("lconv", 16), ("lcb", 4), ("lgb", 16), ("llam", 8), ("fconv", 258), ("fcb", 86),
                        ("alog", 8), ("dtb", 8)):
            off[(l, name)] = o
            o += n
    off["fng"] = o
    o += 16
    return off, o


def _fm(v):
    return np.ascontiguousarray(v.reshape(-1, 128).T)


def _pack_params(I):
    off, n = _prm_layout()
    P = np.zeros((128, n), np.float32)
    for l in range(DEPTH):
        P[:, off[(l, "ada_b")]:off[(l, "ada_b")] + 96] = _fm(I["ada_b"][l])
        P[:, off[(l, "n1g")]:off[(l, "n1g")] + 16] = _fm(I["norm1_g"][l])
        P[:, off[(l, "n2g")]:off[(l, "n2g")] + 16] = _fm(I["norm2_g"][l])
        gc = I["gdn_conv_w"][l]
        P[:, off[(l, "gconv")]:off[(l, "gconv")] + 48] = gc.reshape(4, 12, 128).transpose(2, 1, 0).reshape(128, 48)
        P[:, off[(l, "gng")]] = I["gdn_norm_g"][l]
        P[:, off[(l, "qng")]] = I["q_norm_g"][l]
        P[:, off[(l, "kng")]] = I["k_norm_g"][l]
        lc = I["lru_conv_w"][l]
        P[:, off[(l, "lconv")]:off[(l, "lconv")] + 16] = lc.reshape(4, 4, 128).transpose(2, 1, 0).reshape(128, 16)
        P[:, off[(l, "lcb")]:off[(l, "lcb")] + 4] = _fm(I["lru_conv_b"][l])
        P[:, off[(l, "lgb")]:off[(l, "lgb")] + 16] = I["lru_gate_b"][l].reshape(4, 4, 128).transpose(2, 0, 1).reshape(128, 16)
        P[:, off[(l, "llam")]:off[(l, "llam")] + 8] = I["lru_lambda"][l].reshape(2, 4, 128).transpose(2, 0, 1).reshape(128, 8)
        fc = I["ffn_conv_w"][l]
        P[:, off[(l, "fconv")]:off[(l, "fconv")] + 258] = fc.reshape(3, 86, 128).transpose(2, 1, 0).reshape(128, 258)
        P[:, off[(l, "fcb")]:off[(l, "fcb")] + 86] = _fm(I["ffn_conv_b"][l])
        P[:, off[(l, "alog")]:off[(l, "alog")] + 8] = np.broadcast_to(I["gdn_a_log"][l].reshape(1, 8), (128, 8))
        P[:, off[(l, "dtb")]:off[(l, "dtb")] + 8] = np.broadcast_to(I["gdn_dt_bias"][l].reshape(1, 8), (128, 8))
    P[:, off["fng"]:off["fng"] + 16] = _fm(I["final_norm_g"])
    return P


def _pack_lgw(I):
    g = I["lru_gate_w"]
    out = np.zeros((DEPTH, 2, 2, 4, 128, 128), np.float32)
    for pc in range(4):
        out[:, :, :, pc, 0:64, 0:64] = g[:, :, :, 2 * pc]
        out[:, :, :, pc, 64:128, 64:128] = g[:, :, :, 2 * pc + 1]
    return out


class Builder:
    def __init__(self, n_layers=DEPTH, debug=False, stop_after=None, lite=False, feed=(), only=None, gdn_level=3, gdn_steps=NCH):
        self.lite = lite
        self.feed = set(feed)
        self.only = only
        self.gdn_level = gdn_level
        self.gdn_steps = gdn_steps
        self.p = Prog()
        self.nc = self.p.nc
        self.n_layers = n_layers
        self.debug = debug
        self.stop_after = stop_after
        self.cst_arr, self.coff = _consts()
        self.poff, self.nprm = _prm_layout()
        self.dbg_names = []

    def act(self, out, in_, func, r, w, scale=1.0, bias=0.0):
        self.p.op("act", lambda e: e.activation(out=out, in_=in_, func=func, scale=scale, bias=bias), r, w)

    def tt(self, eng, out, in0, in1, op, r, w):
        self.p.op(eng, lambda e: e.tensor_tensor(out=out, in0=in0, in1=in1, op=op), r, w)

    def ts(self, eng, out, in0, s1, s2, op0, op1, r, w):
        if s2 is None:
            self.p.op(eng, lambda e: e.tensor_scalar(out=out, in0=in0, scalar1=s1, scalar2=None, op0=op0), r, w)
        else:
            self.p.op(eng, lambda e: e.tensor_scalar(out=out, in0=in0, scalar1=s1, scalar2=s2, op0=op0, op1=op1), r, w)

    def stt(self, out, in0, scalar, in1, op0, op1, r, w):
        self.p.op("dve", lambda e: e.scalar_tensor_tensor(out=out, in0=in0, scalar=scalar, in1=in1, op0=op0, op1=op1), r, w)

    def cp(self, eng, out, in_, r, w):
        if eng == "act":
            self.p.op("act", lambda e: e.activation(out=out, in_=in_, func=AF.Copy), r, w)
        else:
            self.p.op(eng, lambda e: e.tensor_copy(out=out, in_=in_), r, w)

    def mm1(self, out, lhsT, rhs, r, w, start=True, stop=True):
        self.p.mm([lambda e: e.matmul(out, lhsT, rhs, start=start, stop=stop)], r, w)

    def sb(self, es, name, shape, dt):
        self._uid = getattr(self, "_uid", 0) + 1
        nm = "sb%d_%s" % (self._uid, name)
        return Buf(es.enter_context(self.nc.sbuf_tensor(nm, shape, dt)), nm)

    def ring(self, es, name, shape, dt, n):
        return Ring([self.sb(es, "%s%d" % (name, i), shape, dt) for i in range(n)])

    def dram(self, name, shape, dt):
        kind = "ExternalOutput" if (self.debug and name in self.debug) else "Internal"
        if name in self.feed:
            kind = "ExternalInput"
        t = self.nc.dram_tensor(name, shape, dt, kind=kind).ap()
        return Buf(t, name)

    def tv(self, buf, ti, kc, w, k0=0, k1=None):
        k1 = kc if k1 is None else k1
        return buf[ti][:, k0 * 512:k1 * 512].rearrange("p (k t) -> p k t", t=512)[:, :, 0:w]

    def C(self, name, rows=slice(0, 128), cols=slice(0, 128)):
        o = self.coff[name]
        return self.cst[:, o:o + 128][rows, cols]

    def prm(self, l, name, j=0, n=1):
        o = self.poff[(l, name)] + j
        return self.PRM[:, o:o + n]

    def build(self):
        nc, p = self.nc, self.p
        def ein(name, shape):
            if self.lite and name in ("ada_w", "w_in", "w_out", "ffn_w_up", "ffn_w_down"):
                shape = [DEPTH, 128, 128]
            return nc.dram_tensor(name, shape, F32, kind="ExternalInput").ap()
        self.xT_in = ein("xT", [D, TOK])
        self.cT_in = ein("cT", [128, KC, 2])
        self.ada_w = ein("ada_w", [DEPTH, D, 6 * D])
        self.w_in = ein("w_in", [DEPTH, D, INC])
        self.w_out = ein("w_out", [DEPTH, D, D])
        self.w_up = ein("ffn_w_up", [DEPTH, D, 2 * FFN])
        self.w_down = ein("ffn_w_down", [DEPTH, FFN, D])
        self.prm_in = ein("prm", [128, self.nprm])
        self.lgw_in = ein("lgw", [DEPTH, 2, 2, 4, 128, 128])
        self.cst_in = ein("cst", [128, self.cst_arr.shape[1]])
        self.rope_in = ein("rope", [2, 128, NLAT])
        self.y_out = Buf(nc.dram_tensor("yT", [D, NLAT], F32, kind="ExternalOutput").ap(), "yT")
        self.xs = self.dram("xs", [D, TOK], F32)
        self.hT = self.dram("hT", [9, 128, KC * 512], BF16)
        self.gq = self.dram("gq", [4, 128, TOK], F32)
        self.gk = self.dram("gk", [4, 128, TOK], F32)
        self.gv = self.dram("gv", [4, 128, TOK], F32)
        self.gz = self.dram("gz", [4, 128, TOK], F32)
        self.gs = self.dram("gs", [16, TOK], F32)
        self.aq = self.dram("aq", [8, 128, TOK], BF16)
        self.ak = self.dram("ak", [2, 128, TOK], BF16)
        self.av = self.dram("av", [2, 128, TOK], BF16)
        self.lx = self.dram("lx", [4, 128, TOK], F32)
        self.lg = self.dram("lg", [4, 128, TOK], F32)
        self.yT = self.dram("ymix", [9, 128, KC * 512], BF16)
        self.of = [self.dram("of%d" % d, [4, 128, TOK], F32) for d in range(2)]
        self.actT = self.dram("actT", [9, 128, 43 * 512], BF16)

        with ExitStack() as gs:
            self.cstb = self.sb(gs, "cst", [128, self.cst_arr.shape[1]], F32)
            self.cst = self.cstb.t
            self.PRMb = self.sb(gs, "prm", [128, self.nprm], F32)
            self.PRM = self.PRMb.t
            self.cbf = self.sb(gs, "cbf", [128, 256], BF16)
            self.mod = self.sb(gs, "mod", [128, 96, 2], F32)
            self.ma = self.sb(gs, "ma", [128, 2, KC, 2], F32)
            self.scT = self.sb(gs, "scT", [128, KC, 2], F32)
            self.epsb = self.sb(gs, "epsb", [128, 1], F32)
            self.PS = [Buf(gs.enter_context(nc.psum_tensor("ps%d" % i, [128, 512], F32)), "ps%d" % i) for i in range(8)]
            p.dma("sp", self.cst[:, :], self.cst_in, w=[self.cstb.k()])
            p.dma("sp", self.PRM[:, :], self.prm_in, w=[self.PRMb.k()])
            p.dma("sp", self.scT[:, :, :], self.cT_in, w=[self.scT.k()])
            p.op("dve", lambda e: e.memset(self.epsb[:, :], EPS), w=[self.epsb.k()])
            self.cp("dve", self.cbf[:, 0:128], self.C("ident"), [self.cstb.k()], [self.cbf.k()])
            self.cp("dve", self.cbf[:, 128:256], self.C("ones"), [self.cstb.k()], [self.cbf.k()])
            self.act(self.scT[:, :, :], self.scT[:, :, :], AF.Silu, [self.scT.k()], [self.scT.k()])
            for ci in range(KC):
                p.dma("sp", self.xs[ci * 128:(ci + 1) * 128, :], self.xT_in[ci * 128:(ci + 1) * 128, :], w=[self.xs.k(ci)])
            p.barrier()
            stages = ["adaln", "norm1", "win", "attn", "lru", "gdn", "wout", "norm2", "ffnup", "ffndown"]
            if self.only:
                stages = list(self.only)
            def run_stage(fn, *a):
                with ExitStack() as es:
                    self.stage_es = es
                    fn(*a)
                    p.barrier()
                    p.emit()

            for l in range(self.n_layers):
                for st in stages:
                    run_stage(getattr(self, "st_" + st), l)
                    if self.stop_after == (l, st):
                        break
                else:
                    continue
                break
            else:
                run_stage(self.st_final)
        return self

    def st_adaln(self, l):
        nc, p = self.nc, self.p
        es = self.stage_es
        if True:
            wr = self.ring(es, "aw", [128, KC, 512], F32, 2)
            for pc in range(24):
                W = wr.get()
                p.dma("sp" if pc % 2 == 0 else "act", W[:, :, :],
                      self.ada_w[l][:, pc * 512:(pc + 1) * 512].rearrange("(k p) n -> p k n", p=128), w=[W.k()])
                for j in range(4):
                    ch = pc * 4 + j
                    P = self.PS[j % 2]
                    fns = [(lambda e, k=k, j=j, W=W, P=P: e.matmul(P[:, 0:2], W[:, k, j * 128:(j + 1) * 128], self.scT[:, k, :],
                                                                    start=(k == 0), stop=(k == KC - 1))) for k in range(KC)]
                    p.mm(fns, r=[W.k(), self.scT.k()], w=[P.k()])
                    self.ts("dve", self.mod[:, ch, :], P[:, 0:2], self.prm(l, "ada_b", ch), None, ALU.add, None,
                            [P.k(), self.PRMb.k()], [self.mod.k()])
            for i, (gname, base) in enumerate((("n1g", 16), ("n2g", 64))):
                for k in range(KC):
                    self.ts("dve", self.ma[:, i, k, :], self.mod[:, base + k, :], 1.0, self.prm(l, gname, k),
                            ALU.add, ALU.mult, [self.mod.k(), self.PRMb.k()], [self.ma.k()])

    def st_norm1(self, l):
        self._norm(l, 0, 0)

    def st_norm2(self, l):
        self._norm(l, 1, 48)

    def _norm(self, l, which, shbase):
        p = self.p
        es = self.stage_es
        if True:
            xr = self.ring(es, "nx", [128, KC, 512], F32, 2)
            sq = self.sb(es, "nsq", [128, KC, 512], F32)
            hr = self.ring(es, "nh", [128, KC, 512], BF16, 2)
            ss = self.sb(es, "nss", [128, 512], F32)
            rs = self.sb(es, "nrs", [128, 512], F32)
            for ti, (t0, w) in enumerate(TILES):
                col = 0 if t0 < NLAT else 1
                X = xr.get()
                H = hr.get()
                p.dma("sp", X[:, :, 0:w], self.xs[:, t0:t0 + w].rearrange("(k p) t -> p k t", p=128),
                      r=[self.xs.k(ci) for ci in range(KC)], w=[X.k()])
                self.act(sq[:, :, 0:w], X[:, :, 0:w], AF.Square, [X.k()], [sq.k()])
                p.op("dve", lambda e: e.tensor_reduce(out=ss[:, 0:w], in_=sq[:, :, 0:w].rearrange("p k t -> p t k"),
                                                      axis=AX.X, op=ALU.add), [sq.k()], [ss.k()])
                P = self.PS[ti % 2]
                self.mm1(P[:, 0:w], self.C("ones"), ss[:, 0:w], [ss.k(), self.cstb.k()], [P.k()])
                self.act(rs[:, 0:w], P[:, 0:w], AF.Ln, [P.k(), self.epsb.k()], [rs.k()], scale=1.0 / D, bias=self.epsb[:, 0:1])
                self.act(rs[:, 0:w], rs[:, 0:w], AF.Exp, [rs.k()], [rs.k()], scale=-0.5)
                self.tt("dve", sq[:, :, 0:w], X[:, :, 0:w], rs[:, 0:w].unsqueeze(1).to_broadcast([128, KC, w]), ALU.mult,
                        [X.k(), rs.k()], [sq.k()])
                for k in range(KC):
                    self.act(H[:, k, 0:w], sq[:, k, 0:w], AF.Identity, [sq.k(), self.ma.k(), self.mod.k()], [H.k()],
                             scale=self.ma[:, which, k, col:col + 1], bias=self.mod[:, shbase + k, col:col + 1])
                p.dma("pool", self.tv(self.hT, ti, KC, w), H[:, :, 0:w], r=[H.k()], w=[self.hT.k(ti)])

    def gemm(self, es, in_dram, kc, groups, wsrc, evac, tiles=TILES, in_ring=None, w_ring=None, pre_tile=None, post_group=None):
        p = self.p
        it = 0
        pend = [None]

        def advance(k):
            for _ in range(k):
                if pend[0] is None:
                    return
                try:
                    next(pend[0])
                except StopIteration:
                    pend[0] = None

        for gi, grp in enumerate(groups):
            Ws = []
            for j, pc in enumerate(grp):
                W = w_ring.get()
                src, ncols = wsrc(pc)
                srcv = src.rearrange("(k p) n -> p k n", p=128)
                for k0 in range(0, kc, 16):
                    k1 = min(kc, k0 + 16)
                    p.dma("pool", W[:, k0:k1, 0:ncols], srcv[:, k0:k1, :], w=[W.k()])
                Ws.append((W, ncols))
            for ti, (t0, w) in enumerate(tiles):
                A = in_ring.get()
                for k0 in range(0, kc, 16):
                    k1 = min(kc, k0 + 16)
                    p.dma("sp", A[:, k0:k1, 0:w], self.tv(in_dram, ti, kc, w, k0, k1), r=[in_dram.k(ti)], w=[A.k()])
                if pre_tile is not None:
                    pre_tile(gi, grp, ti, t0, w)
                for j, pc in enumerate(grp):
                    W, ncols = Ws[j]
                    P = self.PS[it % 4]
                    it += 1
                    fns = [(lambda e, k=k, W=W, A=A, P=P, ncols=ncols, w=w: e.matmul(
                        P[0:ncols, 0:w], W[:, k, 0:ncols], A[:, k, 0:w], start=(k == 0), stop=(k == kc - 1))) for k in range(kc)]
                    p.mm(fns, r=[W.k(), A.k()], w=[P.k()])
                    evac(gi, j, pc, ti, t0, w, P)
                advance(4)
            advance(1 << 30)
            if post_group is not None:
                r_ = post_group(gi, grp)
                pend[0] = r_ if hasattr(r_, "__next__") else None
                advance(1 << 30)
        advance(1 << 30)

    def pnorm(self, src_ap, src_keys, w, scale, sqb, rsb, Pt):
        p = self.p
        self.act(sqb[:, 0:w], src_ap, AF.Square, src_keys, [sqb.k()])
        self.mm1(Pt[:, 0:w], self.C("ones"), sqb[:, 0:w], [sqb.k(), self.cstb.k()], [Pt.k()])
        self.act(rsb[:, 0:w], Pt[:, 0:w], AF.Ln, [Pt.k(), self.epsb.k()], [rsb.k()], scale=scale, bias=self.epsb[:, 0:1])
        self.act(rsb[:, 0:w], rsb[:, 0:w], AF.Exp, [rsb.k()], [rsb.k()], scale=-0.5)

    def conv4(self, out, U, wcol, bias=None):
        p = self.p
        n = UW - 4
        if bias is None:
            self.ts("dve", out[:, 2:2 + n], U[:, 2:2 + n], wcol(2), None, ALU.mult, None, [U.k(), self.PRMb.k()], [out.k()])
        else:
            self.ts("dve", out[:, 2:2 + n], U[:, 2:2 + n], wcol(2), bias, ALU.mult, ALU.add, [U.k(), self.PRMb.k()], [out.k()])
        for j, sh in ((0, -2), (1, -1), (3, 1)):
            self.stt(out[:, 2:2 + n], U[:, 2 + sh:2 + sh + n], wcol(j), out[:, 2:2 + n], ALU.mult, ALU.add,
                     [U.k(), out.k(), self.PRMb.k()], [out.k()])

    def st_win(self, l):
        p, nc = self.p, self.nc
        es = self.stage_es
        if True:
            in_ring = self.ring(es, "wa", [128, KC, 512], BF16, 2)
            w_ring = self.ring(es, "ww", [128, KC, 128], BF16, 8)
            U = [self.sb(es, "wu%d" % i, [128, UW], F32) for i in range(4)]
            T1 = self.sb(es, "wt1", [128, UW], F32)
            T2 = self.sb(es, "wt2", [128, UW], F32)
            sqb = self.sb(es, "wsq", [128, 512], F32)
            rsb = self.sb(es, "wrs", [128, 512], F32)
            t3r = self.ring(es, "wt3", [128, 512], F32, 2)
            t4r = self.ring(es, "wt4", [128, 512], F32, 2)
            obr = self.ring(es, "wob", [128, 512], BF16, 2)
            rope = self.ring(es, "wrp", [128, 2, 512], F32, 2)
            for u in U + [T1, T2]:
                p.op("pool", lambda e, u=u: e.memset(u[:, :], 0.0), w=[u.k()])
            groups = [[i, i + 1, i + 2, i + 3] for i in range(0, 36, 4)] + [[36]]
            coloff = ([i * 128 for i in range(16)] + [2064 + i * 128 for i in range(8)] +
                      [3088, 3216, 3344, 3472] + [3600 + i * 128 for i in range(4)] + [4112 + i * 128 for i in range(4)] + [2048])

            def wsrc(pc):
                n = 16 if pc == 36 else 128
                return self.w_in[l][:, coloff[pc]:coloff[pc] + n], n

            cnt = [0]

            def evac(gi, j, pc, ti, t0, w, P):
                n = 16 if pc == 36 else 128
                u = U[j]
                c0 = ucol(t0)
                if cnt[0] % 2 == 0:
                    self.cp("act", u[0:n, c0:c0 + w], P[0:n, 0:w], [P.k()], [u.k()])
                else:
                    self.cp("dve", u[0:n, c0:c0 + w], P[0:n, 0:w], [P.k()], [u.k()])
                cnt[0] += 1

            def store_seq(dst, idx, src, eng="sp"):
                p.dma(eng, dst[idx][:, 0:NLAT], src[:, 2:2 + NLAT], r=[src.k()], w=[dst.k(idx, 0)])
                p.dma(eng, dst[idx][:, NLAT:TOK], src[:, NLAT + 6:NLAT + 6 + NCTX], r=[src.k()], w=[dst.k(idx, 1)])

            def post_group(gi, grp):
                for j, pc in enumerate(grp):
                    u = U[j]
                    if pc < 12:
                        kind, h = pc // 4, pc % 4
                        self.conv4(T1, u, lambda tap, pc=pc: self.prm(l, "gconv", pc * 4 + tap))
                        self.act(T2[:, :], T1[:, :], AF.Silu, [T1.k()], [T2.k()])
                        yield
                        if kind == 2:
                            store_seq(self.gv, h, T2)
                            yield
                        else:
                            for ti, (t0, w) in enumerate(TILES):
                                c0 = ucol(t0)
                                Pt = self.PS[4 + ti % 2]
                                self.pnorm(T2[:, c0:c0 + w], [T2.k()], w, 1.0, sqb, rsb, Pt)
                                t3 = t3r.get()
                                if kind == 0:
                                    self.stt(t3[:, 0:w], T2[:, c0:c0 + w], HD ** -0.5, rsb[:, 0:w], ALU.mult, ALU.mult,
                                             [T2.k(), rsb.k()], [t3.k()])
                                else:
                                    self.tt("dve", t3[:, 0:w], T2[:, c0:c0 + w], rsb[:, 0:w], ALU.mult, [T2.k(), rsb.k()], [t3.k()])
                                dst = self.gq if kind == 0 else self.gk
                                p.dma("sp", dst[h][:, t0:t0 + w], t3[:, 0:w], r=[t3.k()], w=[dst.k(h, ti)])
                                yield
                    elif pc < 16:
                        self.act(T2[:, :], u[:, :], AF.Silu, [u.k()], [T2.k()])
                        store_seq(self.gz, pc - 12, T2)
                        yield
                    elif pc < 26:
                        isq = pc < 24
                        h = pc - 16 if isq else pc - 24
                        dst = self.aq if isq else self.ak
                        gcol = self.prm(l, "qng" if isq else "kng")
                        for ti, (t0, w) in enumerate(TILES):
                            c0 = ucol(t0)
                            Pt = self.PS[4 + ti % 2]
                            self.pnorm(u[:, c0:c0 + w], [u.k()], w, 1.0 / HD, sqb, rsb, Pt)
                            t3 = t3r.get()
                            ob = obr.get()
                            self.stt(t3[:, 0:w], u[:, c0:c0 + w], gcol, rsb[:, 0:w], ALU.mult, ALU.mult,
                                     [u.k(), rsb.k(), self.PRMb.k()], [t3.k()])
                            if t0 < NLAT:
                                rp = rope.get()
                                p.dma("act", rp[:, :, 0:w], self.rope_in[:, :, t0:t0 + w].rearrange("a p t -> p a t"), w=[rp.k()])
                                Pr = self.PS[6 + ti % 2]
                                self.mm1(Pr[:, 0:w], self.C("perm"), t3[:, 0:w], [t3.k(), self.cstb.k()], [Pr.k()])
                                t4 = t4r.get()
                                self.tt("dve", t4[:, 0:w], Pr[:, 0:w], rp[:, 1, 0:w], ALU.mult, [Pr.k(), rp.k()], [t4.k()])
                                self.tt("pool", t3[:, 0:w], t3[:, 0:w], rp[:, 0, 0:w], ALU.mult, [t3.k(), rp.k()], [t3.k()])
                                self.tt("dve", ob[:, 0:w], t3[:, 0:w], t4[:, 0:w], ALU.add, [t3.k(), t4.k()], [ob.k()])
                            else:
                                self.cp("dve", ob[:, 0:w], t3[:, 0:w], [t3.k()], [ob.k()])
                            p.dma("sp", dst[h][:, t0:t0 + w], ob[:, 0:w], r=[ob.k()], w=[dst.k(h, ti)])
                            yield
                    elif pc < 28:
                        h = pc - 26
                        for ti, (t0, w) in enumerate(TILES):
                            c0 = ucol(t0)
                            ob = obr.get()
                            self.cp("act", ob[:, 0:w], u[:, c0:c0 + w], [u.k()], [ob.k()])
                            p.dma("sp", self.av[h][:, t0:t0 + w], ob[:, 0:w], r=[ob.k()], w=[self.av.k(h, ti)])
                            yield
                    elif pc < 32:
                        h = pc - 28
                        self.conv4(T1, u, lambda tap, h=h: self.prm(l, "lconv", h * 4 + tap), bias=self.prm(l, "lcb", h))
                        store_seq(self.lx, h, T1)
                        yield
                    elif pc < 36:
                        h = pc - 32
                        self.act(T2[:, :], u[:, :], AF.Gelu, [u.k()], [T2.k()])
                        store_seq(self.lg, h, T2)
                        yield
                    else:
                        p.dma("sp", self.gs[:, 0:NLAT], u[0:16, 2:2 + NLAT], r=[u.k()], w=[self.gs.k(0)])
                        p.dma("sp", self.gs[:, NLAT:TOK], u[0:16, NLAT + 6:NLAT + 6 + NCTX], r=[u.k()], w=[self.gs.k(1)])
                        yield

            self.gemm(es, self.hT, KC, groups, wsrc, evac, in_ring=in_ring, w_ring=w_ring, post_group=post_group)

    def st_attn(self, l):
        p = self.p
        scale = HD ** -0.5
        es = self.stage_es
        if True:
            KT = self.sb(es, "akt", [128, TOK], BF16)
            VT = self.sb(es, "avt", [128, TOK], BF16)
            VK = self.sb(es, "avk", [128, NCH, 128], BF16)
            qr = self.ring(es, "aqt", [128, 512], BF16, 2)
            ptr = self.ring(es, "apt", [128, 512], BF16, 4)
            rd = self.sb(es, "ard", [128, 512], F32)
            yr = self.ring(es, "ayo", [128, 512], BF16, 2)
            identb = self.cbf[:, 0:128]
            onesb = self.cbf[:, 128:256]
            for kv in range(2):
                p.dma("sp", KT[:, :], self.ak[kv], r=[self.ak.k(kv, ti) for ti in range(9)], w=[KT.k()])
                p.dma("sp", VT[:, :], self.av[kv], r=[self.av.k(kv, ti) for ti in range(9)], w=[VT.k()])
                for c in range(NCH):
                    P = self.PS[c % 2]
                    self.mm1(P[:, 0:128], VT[:, c * 128:(c + 1) * 128], identb, [VT.k(), self.cbf.k()], [P.k()])
                    self.cp("act" if c % 2 == 0 else "dve", VK[:, c, :], P[:, 0:128], [P.k()], [VK.k()])
                it = 0
                for hq in range(4):
                    h = kv * 4 + hq
                    for ti, (t0, w) in enumerate(TILES):
                        Q = qr.get()
                        p.dma("sp", Q[:, 0:w], self.aq[h][:, t0:t0 + w], r=[self.aq.k(h, ti)], w=[Q.k()])
                        chunks = list(range(NCH)) if t0 < NLAT else [32, 33]
                        PO = self.PS[4 + it % 2]
                        PD = self.PS[6 + it % 2]
                        it += 1
                        LOOK = 2
                        pts = {}
                        nch = len(chunks)
                        for ci in range(nch + LOOK):
                            if ci < nch:
                                c = chunks[ci]
                                Psc = self.PS[ci % 4]
                                self.mm1(Psc[:, 0:w], KT[:, c * 128:(c + 1) * 128], Q[:, 0:w], [KT.k(), Q.k()], [Psc.k()])
                                PT = ptr.get()
                                self.act(PT[:, 0:w], Psc[:, 0:w], AF.Exp, [Psc.k()], [PT.k()], scale=scale)
                                pts[ci] = PT
                            cj = ci - LOOK
                            if cj >= 0:
                                c = chunks[cj]
                                PT = pts.pop(cj)
                                first, last = cj == 0, cj == nch - 1
                                p.mm([lambda e, c=c, PT=PT, PO=PO, first=first, last=last, w=w: e.matmul(
                                         PO[:, 0:w], VK[:, c, :], PT[:, 0:w], start=first, stop=last),
                                      lambda e, PT=PT, PD=PD, first=first, last=last, w=w: e.matmul(
                                         PD[:, 0:w], onesb, PT[:, 0:w], start=first, stop=last)],
                                     r=[VK.k(), PT.k(), self.cbf.k()], w=[PO.k(), PD.k()])
                        p.op("dve", lambda e, PD=PD, w=w: e.reciprocal(out=rd[:, 0:w], in_=PD[:, 0:w]), [PD.k()], [rd.k()])
                        Y = yr.get()
                        self.tt("dve", Y[:, 0:w], PO[:, 0:w], rd[:, 0:w], ALU.mult, [PO.k(), rd.k()], [Y.k()])
                        p.dma("sp", self.yT[ti][:, (4 + h) * 512:(4 + h) * 512 + w], Y[:, 0:w], r=[Y.k()], w=[self.yT.k(4 + h, ti)])

    def st_lru(self, l):
        p = self.p
        es = self.stage_es
        if True:
            X = self.sb(es, "lX", [128, TOK], F32)
            G = self.sb(es, "lG", [128, TOK], F32)
            HF = self.sb(es, "lHF", [128, TOK], F32)
            HB = self.sb(es, "lHB", [128, TOK], F32)
            Wg = self.sb(es, "lW", [128, 4, 128], F32)
            c1 = self.sb(es, "lc1", [128, 4], F32)
            tr = {n: self.ring(es, "l" + n, [128, 512], F32, 2) for n in ("r", "i", "a", "s", "u")}
            yr = self.ring(es, "ly", [128, 512], BF16, 2)
            for pc in range(4):
                p.dma("sp", X[:, :], self.lx[pc], r=[self.lx.k(pc, 0), self.lx.k(pc, 1)], w=[X.k()])
                p.dma("act", G[:, :], self.lg[pc], r=[self.lg.k(pc, 0), self.lg.k(pc, 1)], w=[G.k()])
                for d in range(2):
                    for g in range(2):
                        p.dma("sp", Wg[:, d * 2 + g, :], self.lgw_in[l, d, g, pc], w=[Wg.k()])
                    lam = self.prm(l, "llam", d * 4 + pc)
                    self.act(c1[:, 2 * d:2 * d + 1], lam, AF.Exp, [self.PRMb.k()], [c1.k()], scale=-1.0)
                    self.act(c1[:, 2 * d:2 * d + 1], c1[:, 2 * d:2 * d + 1], AF.Ln, [c1.k()], [c1.k()], bias=1.0)
                    self.ts("dve", c1[:, 2 * d + 1:2 * d + 2], c1[:, 2 * d:2 * d + 1], -16.0, None, ALU.mult, None, [c1.k()], [c1.k()])
                    self.ts("dve", c1[:, 2 * d:2 * d + 1], c1[:, 2 * d:2 * d + 1], -8.0, None, ALU.mult, None, [c1.k()], [c1.k()])
                for d in range(2):
                    order = [8] + (list(range(8)) if d == 0 else list(range(7, -1, -1)))
                    Hd = HF if d == 0 else HB
                    prev = None
                    for ti in order:
                        t0, w = TILES[ti]
                        Pr = self.PS[0 + ti % 2]
                        Pi = self.PS[2 + ti % 2]
                        self.mm1(Pr[:, 0:w], Wg[:, d * 2 + 0, :], X[:, t0:t0 + w], [Wg.k(), X.k()], [Pr.k()])
                        self.mm1(Pi[:, 0:w], Wg[:, d * 2 + 1, :], X[:, t0:t0 + w], [Wg.k(), X.k()], [Pi.k()])
                        R, I_, A, S, Uu = (tr[n].get() for n in ("r", "i", "a", "s", "u"))
                        self.act(R[:, 0:w], Pr[:, 0:w], AF.Sigmoid, [Pr.k(), self.PRMb.k()], [R.k()], bias=self.prm(l, "lgb", (d * 2 + 0) * 4 + pc))
                        self.act(I_[:, 0:w], Pi[:, 0:w], AF.Sigmoid, [Pi.k(), self.PRMb.k()], [I_.k()], bias=self.prm(l, "lgb", (d * 2 + 1) * 4 + pc))
                        self.act(A[:, 0:w], R[:, 0:w], AF.Exp, [R.k(), c1.k()], [A.k()], scale=c1[:, 2 * d:2 * d + 1])
                        self.act(S[:, 0:w], R[:, 0:w], AF.Exp, [R.k(), c1.k()], [S.k()], scale=c1[:, 2 * d + 1:2 * d + 2])
                        self.act(S[:, 0:w], S[:, 0:w], AF.Sqrt, [S.k()], [S.k()], scale=-1.0, bias=1.0)
                        self.tt("pool", Uu[:, 0:w], I_[:, 0:w], X[:, t0:t0 + w], ALU.mult, [I_.k(), X.k()], [Uu.k()])
                        self.tt("dve", Uu[:, 0:w], Uu[:, 0:w], S[:, 0:w], ALU.mult, [Uu.k(), S.k()], [Uu.k()])
                        if prev is None:
                            init = 0.0
                        else:
                            pt0, pw = TILES[prev]
                            init = Hd[:, pt0 + pw - 1:pt0 + pw] if d == 0 else Hd[:, pt0:pt0 + 1]
                        if d == 0:
                            p.op("dve", lambda e, A=A, Uu=Uu, init=init, t0=t0, w=w: e.tensor_tensor_scan(
                                out=Hd[:, t0:t0 + w], data0=A[:, 0:w], data1=Uu[:, 0:w], initial=init, op0=ALU.mult, op1=ALU.add),
                                 [A.k(), Uu.k(), Hd.k(prev)], [Hd.k(ti)])
                        else:
                            p.op("dve", lambda e, A=A, Uu=Uu, init=init, t0=t0, w=w: e.tensor_tensor_scan(
                                out=Hd[:, t0:t0 + w][:, ::-1], data0=A[:, 0:w][:, ::-1], data1=Uu[:, 0:w][:, ::-1], initial=init,
                                op0=ALU.mult, op1=ALU.add), [A.k(), Uu.k(), Hd.k(prev)], [Hd.k(ti)])
                        prev = ti
                for ti, (t0, w) in enumerate(TILES):
                    Y = yr.get()
                    S = tr["s"].get()
                    self.tt("pool", S[:, 0:w], HF[:, t0:t0 + w], HB[:, t0:t0 + w], ALU.add, [HF.k(ti), HB.k(ti)], [S.k()])
                    self.tt("dve", Y[:, 0:w], S[:, 0:w], G[:, t0:t0 + w], ALU.mult, [S.k(), G.k()], [Y.k()])
                    p.dma("sp", self.yT[ti][:, (12 + pc) * 512:(12 + pc) * 512 + w], Y[:, 0:w], r=[Y.k()], w=[self.yT.k(12 + pc, ti)])

    def st_gdn(self, l):
        p = self.p
        es = self.stage_es
        if True:
            GS = self.sb(es, "gGS", [16, TOK], F32)
            SM = self.sb(es, "gSM", [128, NCH, 16], F32)
            BT = self.sb(es, "gBT", [128, NCH, 8], F32)
            NB = self.sb(es, "gNB", [128, NCH, 8], F32)
            LA = self.sb(es, "gLA", [128, NCH, 8], F32)
            GT = self.sb(es, "gGT", [128, NCH, 8], F32)
            EG = self.sb(es, "gEG", [128, NCH, 8], F32)
            EX = self.sb(es, "gEX", [128, NCH, 8], F32)
            NBEG = self.sb(es, "gNBEG", [128, NCH, 8], F32)
            nea = self.sb(es, "gnea", [128, 8], F32)
            S = [self.sb(es, "gS%d" % i, [128, 128], F32) for i in range(8)]
            Sb = [self.sb(es, "gSb%d" % i, [128, 128], BF16) for i in range(8)]
            def mk_rings(tag):
                R = lambda name, dt=F32, n=3: self.ring(es, "g%s%d" % (name, tag), [128, 128], dt, n)
                return (R("q", n=1), R("k", n=1), R("v", n=1), R("T1", n=3), R("D", n=1), R("DS", n=1), R("DT", n=1), R("EGr", n=1),
                        R("Qm", n=3), R("Pm", n=3), R("Rm", n=3), R("Po", n=1), R("vb", n=1), R("kd", n=1), R("at", n=1), R("qd", n=1),
                        R("r", n=2), R("vn", n=2), R("o", n=1))

            RS = [mk_rings(j_) for j_ in range(8)]
            class _PB:
                def __init__(self, b):
                    self.b = b

                def __getitem__(self, idx):
                    return self.b[idx]

                def k(self, *a):
                    return self.b.k()

            pst = [(_PB(self.PS[i]), 0) for i in range(8)]
            psi = [0]

            def PT():
                b, o = pst[psi[0] % 8]
                psi[0] += 1
                return b, o

            p.dma("sp", GS[:, :], self.gs[:, :], r=[self.gs.k(0), self.gs.k(1)], w=[GS.k()])
            for n in range(NCH):
                P = self.PS[n % 2]
                self.mm1(P[:, 0:16], GS[0:16, n * 128:(n + 1) * 128], self.C("ident", slice(0, 16), slice(0, 16)),
                         [GS.k(), self.cstb.k()], [P.k()])
                self.cp("act" if n % 2 == 0 else "dve", SM[:, n, :], P[:, 0:16], [P.k()], [SM.k()])
            self.act(BT[:, :, :], SM[:, :, 0:8], AF.Sigmoid, [SM.k()], [BT.k()])
            self.ts("dve", NB[:, :, :], BT[:, :, :], -1.0, None, ALU.mult, None, [BT.k()], [NB.k()])
            self.act(nea[:, :], self.prm(l, "alog", 0, 8), AF.Exp, [self.PRMb.k()], [nea.k()])
            self.ts("dve", nea[:, :], nea[:, :], -1.0, None, ALU.mult, None, [nea.k()], [nea.k()])
            self.tt("dve", LA[:, :, :], SM[:, :, 8:16], self.prm(l, "dtb", 0, 8).unsqueeze(1).to_broadcast([128, NCH, 8]), ALU.add,
                    [SM.k(), self.PRMb.k()], [LA.k()])
            self.act(LA[:, :, :], LA[:, :, :], AF.Exp, [LA.k()], [LA.k()])
            self.act(LA[:, :, :], LA[:, :, :], AF.Ln, [LA.k()], [LA.k()], bias=1.0)
            self.tt("dve", LA[:, :, :], LA[:, :, :], nea[:, :].unsqueeze(1).to_broadcast([128, NCH, 8]), ALU.mult, [LA.k(), nea.k()], [LA.k()])
            if self.debug and "dbgLA" in self.debug:
                dl = self.dram("dbgLA", [128, NCH * 8], F32)
                p.dma("sp", dl[:, :], LA[:, :, :].rearrange("p n j -> p (n j)"), r=[LA.k()], w=[dl.k()])
            for n in range(NCH):
                P = self.PS[n % 2]
                fns = []
                for d in range(2):
                    inct = self.C("inctF" if d == 0 else "inctB")
                    strm = self.C("strF" if d == 0 else "strB")
                    fns.append(lambda e, d=d, n=n, P=P, inct=inct: e.matmul(P[:, d * 4:d * 4 + 4], inct, LA[:, n, d * 4:d * 4 + 4], start=True, stop=True))
                    fns.append(lambda e, d=d, n=n, P=P, strm=strm: e.matmul(P[:, 8 + d * 4:8 + d * 4 + 4], strm, LA[:, n, d * 4:d * 4 + 4], start=True, stop=True))
                p.mm(fns, r=[LA.k(), self.cstb.k()], w=[P.k()])
                self.cp("dve", GT[:, n, :], P[:, 0:8], [P.k()], [GT.k()])
                self.ts("dve", EG[:, n, :], P[:, 0:8], -BIG, None, ALU.max, None, [P.k()], [EG.k()])
                self.ts("dve", EX[:, n, :], P[:, 8:16], -BIG, None, ALU.max, None, [P.k()], [EX.k()])
            self.act(EG[:, :, :], EG[:, :, :], AF.Exp, [EG.k()], [EG.k()])
            self.act(EX[:, :, :], EX[:, :, :], AF.Exp, [EX.k()], [EX.k()])
            self.tt("dve", NBEG[:, :, :], NB[:, :, :], EG[:, :, :], ALU.mult, [NB.k(), EG.k()], [NBEG.k()])
            for i in range(8):
                p.op("dve", lambda e, i=i: e.memset(S[i][:, :], 0.0), w=[S[i].k()])
            ident = self.C("ident")
            orderF = [32, 33] + list(range(32))
            orderB = [33, 32] + list(range(31, -1, -1))
            ev = [0]

            def evac(out, P, o, r, w):
                ev[0] += 1
                self.cp("act" if ev[0] % 2 == 0 else "dve", out, P[:, o:o + 128], r, w)

            def inst(step, d, h):
                n = (orderF if d == 0 else orderB)[step]
                cs = slice(n * 128, (n + 1) * 128)
                ti = min(n // 4, 8)
                sfx = "F" if d == 0 else "B"
                j = d * 4 + h
                (rq, rk, rv, rT1, rD, rDS, rDT, rEGr, rQ, rP, rR, rPo, rvb, rkd, rat, rqd, rr, rvn, ro) = RS[j]
                q, k, v = rq.get(), rk.get(), rv.get()
                p.dma("sp", q[:, :], self.gq[h][:, cs], r=[self.gq.k(h, ti)], w=[q.k()])
                p.dma("act", k[:, :], self.gk[h][:, cs], r=[self.gk.k(h, ti)], w=[k.k()])
                p.dma("sp", v[:, :], self.gv[h][:, cs], r=[self.gv.k(h, 0), self.gv.k(h, 1)], w=[v.k()])
                qb, kb = q, k
                if self.gdn_level < 1.1:
                    return
                Pg, og = PT()
                self.mm1(Pg[:, og:og + 128], LA[:, n, j:j + 1].to_broadcast([128, 128]), self.C("inct" + sfx),
                         [LA.k(), self.cstb.k()], [Pg.k(og)])
                T1 = rT1.get(); Dm = rD.get(); DS = rDS.get(); DT = rDT.get(); EGr = rEGr.get()
                if self.gdn_level < 1.02:
                    return
                self.ts("dve", T1[:, :], Pg[:, og:og + 128], GT[:, n, j:j + 1], BIG, ALU.subtract, ALU.min,
                        [Pg.k(og), GT.k()], [T1.k()])
                self.tt("dve", T1[:, :], T1[:, :], self.C("neg" + sfx), ALU.max, [T1.k(), self.cstb.k()], [T1.k()])
                self.act(Dm[:, :], T1[:, :], AF.Exp, [T1.k()], [Dm.k()], scale=-1.0)
                T1b = rT1.get()
                self.ts("dve", T1b[:, :], Pg[:, og:og + 128], GT[:, n, j:j + 1], -BIG, ALU.subtract, ALU.max,
                        [Pg.k(og), GT.k()], [T1b.k()])
                self.tt("dve", T1b[:, :], T1b[:, :], self.C("negt" + sfx), ALU.min, [T1b.k(), self.cstb.k()], [T1b.k()])
                self.act(DT[:, :], T1b[:, :], AF.Exp, [T1b.k()], [DT.k()])
                T1c = rT1.get()
                self.ts("dve", T1c[:, :], Pg[:, og:og + 128], -BIG, None, ALU.max, None, [Pg.k(og)], [T1c.k()])
                self.act(EGr[:, :], T1c[:, :], AF.Exp, [T1c.k()], [EGr.k()])
                self.tt("pool", DS[:, :], Dm[:, :], self.C("offd"), ALU.mult, [Dm.k(), self.cstb.k()], [DS.k()])
                yield
                if self.gdn_level < 1.2:
                    return
                Pk, ok = PT()
                self.mm1(Pk[:, ok:ok + 128], kb[:, :], kb[:, :], [kb.k()], [Pk.k(ok)])
                Q0 = rQ.get()
                self.stt(Q0[:, :], Pk[:, ok:ok + 128], NB[:, n, j:j + 1], DS[:, :], ALU.mult, ALU.mult,
                         [Pk.k(ok), NB.k(), DS.k()], [Q0.k()])
                Pq, oq = PT()
                self.mm1(Pq[:, oq:oq + 128], kb[:, :], qb[:, :], [kb.k(), qb.k()], [Pq.k(oq)])
                at = rat.get()
                self.tt("dve", at[:, :], Pq[:, oq:oq + 128], DT[:, :], ALU.mult, [Pq.k(oq), DT.k()], [at.k()])
                qd = rqd.get()
                self.tt("pool", qd[:, :], q[:, :], EGr[:, :], ALU.mult, [q.k(), EGr.k()], [qd.k()])
                yield
                if self.gdn_level < 1.3:
                    return
                Pa, oa = PT()
                self.mm1(Pa[:, oa:oa + 128], k[:, :], ident, [k.k(), self.cstb.k()], [Pa.k(oa)])
                kd = rkd.get()
                self.act(kd[:, :], Pa[:, oa:oa + 128], AF.Identity, [Pa.k(oa), EX.k()], [kd.k()], scale=EX[:, n, j:j + 1])
                Pv, ov = PT()
                self.mm1(Pv[:, ov:ov + 128], v[:, :], ident, [v.k(), self.cstb.k()], [Pv.k(ov)])
                vb = rvb.get()
                self.act(vb[:, :], Pv[:, ov:ov + 128], AF.Identity, [Pv.k(ov), BT.k()], [vb.k()], scale=BT[:, n, j:j + 1])
                yield
                if self.gdn_level < 1.4:
                    return
                Pp, op_ = PT()
                self.mm1(Pp[:, op_:op_ + 128], Q0[:, :], ident, [Q0.k(), self.cstb.k()], [Pp.k(op_)])
                Pm = rP.get()
                self.tt("dve", Pm[:, :], Pp[:, op_:op_ + 128], self.C("bd"), ALU.mult, [Pp.k(op_), self.cstb.k()], [Pm.k()])
                Poff = rPo.get()
                self.tt("dve", Poff[:, :], Pp[:, op_:op_ + 128], self.C("offb"), ALU.mult, [Pp.k(op_), self.cstb.k()], [Poff.k()])
                Qb = rQ.get()
                self.tt("pool", Qb[:, :], Q0[:, :], self.C("bd"), ALU.mult, [Q0.k(), self.cstb.k()], [Qb.k()])
                Rm = rR.get()
                self.tt("pool", Rm[:, :], Pm[:, :], ident, ALU.add, [Pm.k(), self.cstb.k()], [Rm.k()])
                yield
                Qm = Qb
                if self.gdn_level < 2:
                    return
                for it in range(1, 6):
                    Qn = rQ.get()
                    Pa2, oa2 = PT()
                    self.mm1(Pa2[:, oa2:oa2 + 128], Pm[:, :], Qm[:, :], [Pm.k(), Qm.k()], [Pa2.k(oa2)])
                    evac(Qn[:, :], Pa2, oa2, [Pa2.k(oa2)], [Qn.k()])
                    yield
                    if it < 5:
                        Pn = rP.get()
                        Pb2, ob2 = PT()
                        self.mm1(Pb2[:, ob2:ob2 + 128], Qm[:, :], Pm[:, :], [Pm.k(), Qm.k()], [Pb2.k(ob2)])
                        evac(Pn[:, :], Pb2, ob2, [Pb2.k(ob2)], [Pn.k()])
                        yield
                    Pc2, oc2 = PT()
                    self.mm1(Pc2[:, oc2:oc2 + 128], Qn[:, :], Rm[:, :], [Qn.k(), Rm.k()], [Pc2.k(oc2)])
                    Rn = rR.get()
                    self.tt("dve", Rn[:, :], Pc2[:, oc2:oc2 + 128], Rm[:, :], ALU.add, [Pc2.k(oc2), Rm.k()], [Rn.k()])
                    yield
                    Qm, Rm = Qn, Rn
                    if it < 5:
                        Pm = Pn
                TT = Rm
                if self.gdn_level < 3:
                    return
                Sj, Sbj = S[j], S[j]
                P1, o1 = PT()
                self.mm1(P1[:, o1:o1 + 128], kb[:, :], Sbj[:, :], [kb.k(), Sbj.k()], [P1.k(o1)])
                rr_ = rr.get()
                self.stt(rr_[:, :], P1[:, o1:o1 + 128], NBEG[:, n, j:j + 1], vb[:, :], ALU.mult, ALU.add,
                         [P1.k(o1), NBEG.k(), vb.k()], [rr_.k()])
                P2a, o2a = PT()
                self.mm1(P2a[:, o2a:o2a + 128], TT[:, :], rr_[:, :], [TT.k(), rr_.k()], [P2a.k(o2a)])
                v1 = rvn.get()
                self.cp("act", v1[:, :], P2a[:, o2a:o2a + 128], [P2a.k(o2a)], [v1.k()])
                yield
                P2b, o2b = PT()
                self.mm1(P2b[:, o2b:o2b + 128], Poff[:, :], v1[:, :], [Poff.k(), v1.k()], [P2b.k(o2b)])
                r2 = rr.get()
                self.tt("dve", r2[:, :], P2b[:, o2b:o2b + 128], rr_[:, :], ALU.add, [P2b.k(o2b), rr_.k()], [r2.k()])
                yield
                P2, o2 = PT()
                self.mm1(P2[:, o2:o2 + 128], TT[:, :], r2[:, :], [TT.k(), r2.k()], [P2.k(o2)])
                vn = rvn.get()
                self.cp("act", vn[:, :], P2[:, o2:o2 + 128], [P2.k(o2)], [vn.k()])
                yield
                P3, o3 = PT()
                p.mm([lambda e, P3=P3, o3=o3, Sbj=Sbj, qd=qd: e.matmul(P3[:, o3:o3 + 128], Sbj[:, :], qd[:, :], start=True, stop=False),
                      lambda e, P3=P3, o3=o3, vn=vn, at=at: e.matmul(P3[:, o3:o3 + 128], vn[:, :], at[:, :], start=False, stop=True)],
                     r=[Sbj.k(), qd.k(), vn.k(), at.k()], w=[P3.k(o3)])
                o_ = ro.get()
                self.cp("dve", o_[:, :], P3[:, o3:o3 + 128], [P3.k(o3)], [o_.k()])
                p.dma("sp", self.of[d][h][:, cs], o_[:, :], r=[o_.k()], w=[self.of[d].k(h, n)])
                yield
                P4, o4 = PT()
                self.mm1(P4[:, o4:o4 + 128], kd[:, :], vn[:, :], [kd.k(), vn.k()], [P4.k(o4)])
                endc = 127 if d == 0 else 0
                self.stt(Sj[:, :], Sj[:, :], EGr[:, endc:endc + 1], P4[:, o4:o4 + 128], ALU.mult, ALU.add,
                         [Sj.k(), EGr.k(), P4.k(o4)], [Sj.k()])

            for step in range(self.gdn_steps if self.gdn_level >= 1 else 0):
                gens = [inst(step, d, h) for d in range(2) for h in range(4)]
                while gens:
                    for g_ in list(gens):
                        try:
                            next(g_)
                        except StopIteration:
                            gens.remove(g_)
            p.barrier()
            fr = {n: self.ring(es, "gf" + n, [128, 512], F32, 2) for n in ("a", "b", "z", "q", "r")}
            yr = self.ring(es, "gfy", [128, 512], BF16, 2)
            for h in range(4):
                for ti, (t0, w) in enumerate(TILES):
                    A, B, Z, SQ, RS = (fr[n].get() for n in ("a", "b", "z", "q", "r"))
                    p.dma("sp", A[:, 0:w], self.of[0][h][:, t0:t0 + w], w=[A.k()])
                    p.dma("act", B[:, 0:w], self.of[1][h][:, t0:t0 + w], w=[B.k()])
                    p.dma("sp", Z[:, 0:w], self.gz[h][:, t0:t0 + w], w=[Z.k()])
                    self.tt("dve", A[:, 0:w], A[:, 0:w], B[:, 0:w], ALU.add, [A.k(), B.k()], [A.k()])
                    Pt = self.PS[ti % 2]
                    self.pnorm(A[:, 0:w], [A.k()], w, 1.0 / HD, SQ, RS, Pt)
                    self.stt(A[:, 0:w], A[:, 0:w], self.prm(l, "gng"), RS[:, 0:w], ALU.mult, ALU.mult,
                             [A.k(), RS.k(), self.PRMb.k()], [A.k()])
                    Y = yr.get()
                    self.tt("dve", Y[:, 0:w], A[:, 0:w], Z[:, 0:w], ALU.mult, [A.k(), Z.k()], [Y.k()])
                    p.dma("sp", self.yT[ti][:, h * 512:h * 512 + w], Y[:, 0:w], r=[Y.k()], w=[self.yT.k(h, ti)])

    def _resid_gemm(self, l, es, in_dram, kc, wfull, gbase, ngrp, in_ring, w_ring):
        p = self.p
        xr = self.ring(es, "rx", [128, 4, 512], F32, 2)
        cur = {}

        def wsrc(pc):
            return wfull[:, pc * 128:(pc + 1) * 128], 128

        def pre_tile(gi, grp, ti, t0, w):
            Xb = xr.get()
            g0 = grp[0]
            p.dma("act", Xb[:, 0:len(grp), 0:w], self.xs[g0 * 128:(g0 + len(grp)) * 128, t0:t0 + w].rearrange("(g p) t -> p g t", p=128),
                  r=[self.xs.k(pc) for pc in grp], w=[Xb.k()])
            cur["x"] = Xb

        def evac(gi, j, pc, ti, t0, w, P):
            Xb = cur["x"]
            col = 0 if t0 < NLAT else 1
            self.stt(Xb[:, j, 0:w], P[:, 0:w], self.mod[:, gbase + pc, col:col + 1], Xb[:, j, 0:w], ALU.mult, ALU.add,
                     [P.k(), self.mod.k(), Xb.k()], [Xb.k()])
            if j == len(cur["grp"]) - 1:
                grp = cur["grp"]
                g0 = grp[0]
                p.dma("act", self.xs[g0 * 128:(g0 + len(grp)) * 128, t0:t0 + w].rearrange("(g p) t -> p g t", p=128),
                      Xb[:, 0:len(grp), 0:w], r=[Xb.k()], w=[self.xs.k(pc) for pc in grp])

        groups = [list(range(g * 4, g * 4 + 4)) for g in range(ngrp)]

        def pre_tile2(gi, grp, ti, t0, w):
            cur["grp"] = grp
            pre_tile(gi, grp, ti, t0, w)

        self.gemm(es, in_dram, kc, groups, wsrc, evac, in_ring=in_ring, w_ring=w_ring, pre_tile=pre_tile2)

    def st_wout(self, l):
        es = self.stage_es
        if True:
            in_ring = self.ring(es, "oa", [128, KC, 512], BF16, 2)
            w_ring = self.ring(es, "ow", [128, KC, 128], BF16, 8)
            self._resid_gemm(l, es, self.yT, KC, self.w_out[l], 32, 4, in_ring, w_ring)

    def st_ffndown(self, l):
        es = self.stage_es
        if True:
            in_ring = self.ring(es, "da", [128, 43, 512], BF16, 2)
            w_ring = self.ring(es, "dw", [128, 43, 128], BF16, 6)
            self._resid_gemm(l, es, self.actT, 43, self.w_down[l], 80, 4, in_ring, w_ring)

    def st_ffnup(self, l):
        p = self.p
        es = self.stage_es
        HT = self.sb(es, "uHT", [128, KC, TOK], BF16)
        w_ring = self.ring(es, "uw", [128, KC, 128], BF16, 4)
        eg_r = self.ring(es, "ueg", [128, 514], F32, 3)
        eu_r = self.ring(es, "ueu", [128, 514], F32, 3)
        tg_r = self.ring(es, "utg", [128, 512], F32, 2)
        tu_r = self.ring(es, "utu", [128, 512], F32, 2)
        o_r = self.ring(es, "uo", [128, 512], BF16, 3)
        for ti, (t0, w) in enumerate(TILES):
            p.dma("sp" if ti % 2 == 0 else "act", HT[:, :, t0:t0 + w], self.tv(self.hT, ti, KC, w), w=[HT.k(ti)])
        it = [0]

        def epilogue(j, E, ti):
            t0, w = TILES[ti]
            outs = []
            for gi, (Eb, ring, pc) in enumerate(((E[0], tg_r, j), (E[1], tu_r, 43 + j))):
                Tt = ring.get()
                wc = lambda tap, pc=pc: self.prm(l, "fconv", pc * 3 + tap)
                self.act(Tt[:, 0:w], Eb[:, 1:1 + w], AF.Identity, [Eb.k(), self.PRMb.k()], [Tt.k()], scale=wc(1), bias=self.prm(l, "fcb", pc))
                self.stt(Tt[:, 0:w], Eb[:, 0:w], wc(0), Tt[:, 0:w], ALU.mult, ALU.add, [Eb.k(), Tt.k(), self.PRMb.k()], [Tt.k()])
                self.stt(Tt[:, 0:w], Eb[:, 2:2 + w], wc(2), Tt[:, 0:w], ALU.mult, ALU.add, [Eb.k(), Tt.k(), self.PRMb.k()], [Tt.k()])
                outs.append(Tt)
            TG, TU = outs
            self.act(TG[:, 0:w], TG[:, 0:w], AF.Silu, [TG.k()], [TG.k()])
            O = o_r.get()
            self.tt("dve", O[:, 0:w], TG[:, 0:w], TU[:, 0:w], ALU.mult, [TG.k(), TU.k()], [O.k()])
            p.dma("sp", self.actT[ti][:, j * 512:j * 512 + w], O[:, 0:w], r=[O.k()], w=[self.actT.k(j, ti)])

        for j in range(43):
            Ws = []
            for pc in (j, 43 + j):
                W = w_ring.get()
                p.dma("pool", W[:, :, :], self.w_up[l][:, pc * 128:(pc + 1) * 128].rearrange("(k p) n -> p k n", p=128), w=[W.k()])
                Ws.append(W)
            prev = None
            for ti, (t0, w) in enumerate(TILES):
                E = (eg_r.get(), eu_r.get())
                for gi in range(2):
                    P = self.PS[it[0] % 4]
                    it[0] += 1
                    W = Ws[gi]
                    fns = [(lambda e, k=k, W=W, P=P, t0=t0, w=w: e.matmul(P[:, 0:w], W[:, k, :], HT[:, k, t0:t0 + w],
                                                                             start=(k == 0), stop=(k == KC - 1))) for k in range(KC)]
                    p.mm(fns, r=[W.k(), HT.k(ti)], w=[P.k()])
                    Eb = E[gi]
                    self.cp("act" if gi == 0 else "dve", Eb[:, 1:1 + w], P[:, 0:w], [P.k()], [Eb.k()])
                    if ti == 0 or ti == 8:
                        p.op("pool", lambda e, Eb=Eb: e.memset(Eb[:, 0:1], 0.0), w=[Eb.k()])
                    else:
                        pw = TILES[prev[1]][1]
                        self.cp("pool", Eb[:, 0:1], prev[0][gi][:, pw:pw + 1], [prev[0][gi].k()], [Eb.k()])
                    if prev is not None:
                        pw = TILES[prev[1]][1]
                        pE = prev[0][gi]
                        if ti == 8:
                            p.op("pool", lambda e, pE=pE, pw=pw: e.memset(pE[:, pw + 1:pw + 2], 0.0), w=[pE.k()])
                        else:
                            self.cp("pool", pE[:, pw + 1:pw + 2], Eb[:, 1:2], [Eb.k()], [pE.k()])
                if prev is not None:
                    epilogue(j, prev[0], prev[1])
                prev = (E, ti)
            for gi in range(2):
                pE = prev[0][gi]
                pw = TILES[prev[1]][1]
                p.op("pool", lambda e, pE=pE, pw=pw: e.memset(pE[:, pw + 1:pw + 2], 0.0), w=[pE.k()])
            epilogue(j, prev[0], prev[1])

    def st_final(self):
        p = self.p
        es = self.stage_es
        if True:
            xr = self.ring(es, "fx", [128, KC, 512], F32, 2)
            sq = self.sb(es, "fsq", [128, KC, 512], F32)
            ss = self.sb(es, "fss", [128, 512], F32)
            rs = self.sb(es, "frs", [128, 512], F32)
            o = self.poff["fng"]
            for ti, (t0, w) in enumerate(TILES[:8]):
                X = xr.get()
                p.dma("sp", X[:, :, :], self.xs[:, t0:t0 + w].rearrange("(k p) t -> p k t", p=128), w=[X.k()])
                self.act(sq[:, :, :], X[:, :, :], AF.Square, [X.k()], [sq.k()])
                p.op("dve", lambda e: e.tensor_reduce(out=ss[:, :], in_=sq[:, :, :].rearrange("p k t -> p t k"), axis=AX.X, op=ALU.add),
                     [sq.k()], [ss.k()])
                P = self.PS[ti % 2]
                self.mm1(P[:, :], self.C("ones"), ss[:, :], [ss.k(), self.cstb.k()], [P.k()])
                self.act(rs[:, :], P[:, :], AF.Sqrt, [P.k(), self.epsb.k()], [rs.k()], scale=1.0 / D, bias=self.epsb[:, 0:1])
                p.op("dve", lambda e: e.reciprocal(out=rs[:, :], in_=rs[:, :]), [rs.k()], [rs.k()])
                self.tt("dve", sq[:, :, :], X[:, :, :], rs[:, :].unsqueeze(1).to_broadcast([128, KC, 512]), ALU.mult, [X.k(), rs.k()], [sq.k()])
                for k in range(KC):
                    self.ts("pool" if k % 2 else "dve", sq[:, k, :], sq[:, k, :], self.PRM[:, o + k:o + k + 1], None, ALU.mult, None,
                            [sq.k(), self.PRMb.k()], [sq.k()])
                p.dma("sp", self.y_out[:, t0:t0 + w].rearrange("(k p) t -> p k t", p=128), sq[:, :, :], r=[sq.k()], w=[self.y_out.k(ti)])


def make_in_maps(I, n_cores, batches):
    cst, _ = _consts()
    prm = _pack_params(I)
    lgw = _pack_lgw(I)
    rope = _rope_tables()
    f = lambda a: np.ascontiguousarray(np.asarray(a, dtype=np.float32))
    shared = dict(ada_w=f(I["ada_w"]), w_in=f(I["w_in"]), w_out=f(I["w_out"]), ffn_w_up=f(I["ffn_w_up"]),
                  ffn_w_down=f(I["ffn_w_down"]), prm=prm, lgw=lgw, cst=cst, rope=rope)
    maps = []
    for b in batches:
        xT = np.ascontiguousarray(np.concatenate([I["x"][b].T, I["ctx"][b].T], axis=1))
        cv = np.stack([I["c"][b], I["c_ctx"]], axis=0)
        cT = np.ascontiguousarray(cv.reshape(2, KC, 128).transpose(2, 1, 0))
        m = dict(shared)
        m["xT"] = xT
        m["cT"] = cT
        maps.append(m)
    return maps


_CACHE = {}


def kernel(**inputs):
    I = {k: np.asarray(v) for k, v in inputs.items()}
    B = I["x"].shape[0]
    if "prog" not in _CACHE:
        _CACHE["prog"] = Builder().build()
    bld = _CACHE["prog"]
    maps = make_in_maps(I, B, list(range(B)))
    res = run_bass_kernel_spmd(bld.nc, maps, core_ids=list(range(B)))
    out = np.stack([np.ascontiguousarray(res.results[b]["yT"].T) for b in range(B)], axis=0)
    return out.astype(np.float32)
```

```python
import math
from contextlib import ExitStack
import numpy as np
import concourse.bass as bass
import concourse.mybir as mybir
from concourse.bass_utils import run_bass_kernel_spmd

F32 = mybir.dt.float32
BF16 = mybir.dt.bfloat16
AF = mybir.ActivationFunctionType
ALU = mybir.AluOpType
AX = mybir.AxisListType

D = 2048
DEPTH = 4
NLAT = 4096
NCTX = 256
TOK = NLAT + NCTX
KC = D // 128
HD = 128
FFN = 5504
INC = 4624
EPS = 1e-6
TILES = [(i * 512, 512) for i in range(8)] + [(NLAT, NCTX)]
UW = TOK + 8
NCH = TOK // 128
BIG = 80.0
SELF_WAITS = True
ALL_INC = True
import os as _os
GX = int(_os.environ.get('GX', '0'))


def ucol(t):
    return t + 2 if t < NLAT else t + 6


class Prog:
    def __init__(self, n_dma_sems=16):
        nc = bass.Bass("TRN2", target_bir_lowering=False)
        self.nc = nc
        self.eng = dict(pe=nc.tensor, act=nc.scalar, dve=nc.vector, pool=nc.gpsimd, sp=nc.sync)
        self.sem = {}
        self.cnt = {}
        for e in ("pe", "act", "dve", "pool"):
            self.sem[e] = nc.alloc_semaphore("s_" + e)
            self.cnt[e] = 0
        self.dq = []
        for i in range(n_dma_sems):
            nm = "d%d" % i
            self.sem[nm] = nc.alloc_semaphore("s_" + nm)
            self.cnt[nm] = 0
            self.dq.append(nm)
        self.dq_rr = 0
        self.waited = {e: {} for e in self.eng}
        self.lastw = {}
        self.reads = {}
        self.n_inst = 0

    def _need(self, r, w):
        need = {}
        for k in r:
            lw = self.lastw.get(k)
            if lw is not None:
                need[lw[0]] = max(need.get(lw[0], 0), lw[1])
        for k in w:
            lw = self.lastw.get(k)
            if lw is not None:
                need[lw[0]] = max(need.get(lw[0], 0), lw[1])
            for e, i in self.reads.get(k, {}).items():
                need[e] = max(need.get(e, 0), i)
        return need

    def _waits(self, issuer, need, skip_self=False):
        wd = self.waited[issuer]
        for s, v in need.items():
            if skip_self and s == issuer:
                continue
            if wd.get(s, 0) >= v:
                continue
            self.eng[issuer].wait_ge(self.sem[s], v)
            wd[s] = v

    def _mark(self, who, idx, r, w):
        for k in w:
            self.lastw[k] = (who, idx)
            self.reads[k] = {}
        for k in r:
            d = self.reads.setdefault(k, {})
            d[who] = max(d.get(who, 0), idx)

    def op(self, e, fn, r=(), w=()):
        need = self._need(r, w)
        self._waits(e, need, skip_self=(e == "pe"))
        ins = fn(self.eng[e])
        ins.then_inc(self.sem[e], 1)
        self.cnt[e] += 1
        self._mark(e, self.cnt[e], r, w)
        self.n_inst += 1

    def mm(self, fns, r=(), w=()):
        need = self._need(r, w)
        self._waits("pe", need, skip_self=True)
        ins = None
        for fn in fns:
            ins = fn(self.eng["pe"])
            self.n_inst += 1
        ins.then_inc(self.sem["pe"], 1)
        self.cnt["pe"] += 1
        self._mark("pe", self.cnt["pe"], r, w)

    def dma(self, issuer, out, in_, r=(), w=()):
        need = self._need(r, w)
        q = self.dq[self.dq_rr]
        self.dq_rr = (self.dq_rr + 1) % len(self.dq)
        if self.cnt[q] > 0:
            need[q] = max(need.get(q, 0), self.cnt[q])
        self._waits(issuer, need)
        ins = self.eng[issuer].dma_start(out=out, in_=in_)
        ins.then_inc(self.sem[q], 16)
        self.cnt[q] += 16
        self._mark(q, self.cnt[q], r, w)
        self.n_inst += 1

    def barrier(self):
        need = {s: v for s, v in self.cnt.items() if v > 0}
        for e in self.eng:
            self._waits(e, dict(need))
        self.lastw.clear()
        self.reads.clear()

    def emit(self):
        pass


class Buf:
    _n = 0

    def __init__(self, t, name=None):
        self.t = t
        Buf._n += 1
        self.id = name or ("b%d" % Buf._n)

    def __getitem__(self, idx):
        return self.t[idx]

    def k(self, *slot):
        return (self.id,) + slot


class Ring:
    def __init__(self, bufs):
        self.b = bufs
        self.i = 0

    def get(self):
        b = self.b[self.i % len(self.b)]
        self.i += 1
        return b


def _consts():
    i = np.arange(128)
    c = {}
    c["ident"] = np.eye(128, dtype=np.float32)
    c["ones"] = np.ones((128, 128), np.float32)
    perm = np.zeros((128, 128), np.float32)
    for d in range(128):
        j = d % 64
        partner = d + 32 if j < 32 else d - 32
        perm[partner, d] = 1.0
    c["perm"] = perm
    incF = (i[:, None] >= i[None, :]).astype(np.float32)
    incB = (i[:, None] <= i[None, :]).astype(np.float32)
    strF = (i[:, None] > i[None, :]).astype(np.float32)
    strB = (i[:, None] < i[None, :]).astype(np.float32)
    c["inctF"] = incF.T.copy(); c["inctB"] = incB.T.copy()
    c["strF"] = strF; c["strB"] = strB
    c["negF"] = (1.0 - incF) * BIG; c["negB"] = (1.0 - incB) * BIG
    c["negtF"] = -c["negF"].T.copy(); c["negtB"] = -c["negB"].T.copy()
    c["offd"] = 1.0 - np.eye(128, dtype=np.float32)
    bd = np.zeros((128, 128), np.float32); bd[:64, :64] = 1.0; bd[64:, 64:] = 1.0
    c["bd"] = bd; c["offb"] = 1.0 - bd
    names = list(c.keys())
    arr = np.concatenate([c[n] for n in names], axis=1)
    off = {n: k * 128 for k, n in enumerate(names)}
    return arr, off


def _rope_tables():
    rows = NLAT // 64
    row = np.repeat(np.arange(rows, dtype=np.float32), 64)
    col = np.tile(np.arange(64, dtype=np.float32), rows)
    inv_freq = (10000.0 ** (-(np.arange(0, 64, 2, dtype=np.float32)) / np.float32(64))).astype(np.float32)
    cosT = np.zeros((128, NLAT), np.float32)
    sinT = np.zeros((128, NLAT), np.float32)
    for d in range(128):
        axis = d // 64
        j = d % 64
        half = j // 32
        f = j % 32
        pos = row if axis == 0 else col
        ang = (pos * inv_freq[f]).astype(np.float32)
        cosT[d] = np.cos(ang)
        sinT[d] = np.sin(ang) * (-1.0 if half == 0 else 1.0)
    return np.stack([cosT, sinT])


def _prm_layout():
    off = {}
    o = 0
    for l in range(DEPTH):
        for name, n in (("ada_b", 96), ("n1g", 16), ("n2g", 16), ("gconv", 48), ("gng", 1), ("qng", 1), ("kng", 1),
                        ("lconv", 16), ("lcb", 4), ("lgb", 16), ("llam", 8), ("fconv", 258), ("fcb", 86),
                        ("alog", 8), ("dtb", 8)):
            off[(l, name)] = o
            o += n
    off["fng"] = o
    o += 16
    return off, o


def _fm(v):
    return np.ascontiguousarray(v.reshape(-1, 128).T)


def _pack_params(I):
    off, n = _prm_layout()
    P = np.zeros((128, n), np.float32)
    for l in range(DEPTH):
        P[:, off[(l, "ada_b")]:off[(l, "ada_b")] + 96] = _fm(I["ada_b"][l])
        P[:, off[(l, "n1g")]:off[(l, "n1g")] + 16] = _fm(I["norm1_g"][l])
        P[:, off[(l, "n2g")]:off[(l, "n2g")] + 16] = _fm(I["norm2_g"][l])
        gc = I["gdn_conv_w"][l]
        P[:, off[(l, "gconv")]:off[(l, "gconv")] + 48] = gc.reshape(4, 12, 128).transpose(2, 1, 0).reshape(128, 48)
        P[:, off[(l, "gng")]] = I["gdn_norm_g"][l]
        P[:, off[(l, "qng")]] = I["q_norm_g"][l]
        P[:, off[(l, "kng")]] = I["k_norm_g"][l]
        lc = I["lru_conv_w"][l]
        P[:, off[(l, "lconv")]:off[(l, "lconv")] + 16] = lc.reshape(4, 4, 128).transpose(2, 1, 0).reshape(128, 16)
        P[:, off[(l, "lcb")]:off[(l, "lcb")] + 4] = _fm(I["lru_conv_b"][l])
        P[:, off[(l, "lgb")]:off[(l, "lgb")] + 16] = I["lru_gate_b"][l].reshape(4, 4, 128).transpose(2, 0, 1).reshape(128, 16)
        P[:, off[(l, "llam")]:off[(l, "llam")] + 8] = I["lru_lambda"][l].reshape(2, 4, 128).transpose(2, 0, 1).reshape(128, 8)
        fc = I["ffn_conv_w"][l]
        P[:, off[(l, "fconv")]:off[(l, "fconv")] + 258] = fc.reshape(3, 86, 128).transpose(2, 1, 0).reshape(128, 258)
        P[:, off[(l, "fcb")]:off[(l, "fcb")] + 86] = _fm(I["ffn_conv_b"][l])
        P[:, off[(l, "alog")]:off[(l, "alog")] + 8] = np.broadcast_to(I["gdn_a_log"][l].reshape(1, 8), (128, 8))
        P[:, off[(l, "dtb")]:off[(l, "dtb")] + 8] = np.broadcast_to(I["gdn_dt_bias"][l].reshape(1, 8), (128, 8))
    P[:, off["fng"]:off["fng"] + 16] = _fm(I["final_norm_g"])
    return P


def _pack_lgw(I):
    g = I["lru_gate_w"]
    out = np.zeros((DEPTH, 2, 2, 4, 128, 128), np.float32)
    for pc in range(4):
        out[:, :, :, pc, 0:64, 0:64] = g[:, :, :, 2 * pc]
        out[:, :, :, pc, 64:128, 64:128] = g[:, :, :, 2 * pc + 1]
    return out


class Builder:
    def __init__(self, n_layers=DEPTH, debug=False, stop_after=None, lite=False, feed=(), only=None, gdn_level=3, gdn_steps=NCH):
        self.lite = lite
        self.feed = set(feed)
        self.only = only
        self.gdn_level = gdn_level
        self.gdn_steps = gdn_steps
        self.p = Prog()
        self.nc = self.p.nc
        self.n_layers = n_layers
        self.debug = debug
        self.stop_after = stop_after
        self.cst_arr, self.coff = _consts()
        self.poff, self.nprm = _prm_layout()
        self.dbg_names = []

    def act(self, out, in_, func, r, w, scale=1.0, bias=0.0):
        self.p.op("act", lambda e: e.activation(out=out, in_=in_, func=func, scale=scale, bias=bias), r, w)

    def tt(self, eng, out, in0, in1, op, r, w):
        self.p.op(eng, lambda e: e.tensor_tensor(out=out, in0=in0, in1=in1, op=op), r, w)

    def ts(self, eng, out, in0, s1, s2, op0, op1, r, w):
        if s2 is None:
            self.p.op(eng, lambda e: e.tensor_scalar(out=out, in0=in0, scalar1=s1, scalar2=None, op0=op0), r, w)
        else:
            self.p.op(eng, lambda e: e.tensor_scalar(out=out, in0=in0, scalar1=s1, scalar2=s2, op0=op0, op1=op1), r, w)

    def stt(self, out, in0, scalar, in1, op0, op1, r, w):
        self.p.op("dve", lambda e: e.scalar_tensor_tensor(out=out, in0=in0, scalar=scalar, in1=in1, op0=op0, op1=op1), r, w)

    def cp(self, eng, out, in_, r, w):
        if eng == "act":
            self.p.op("act", lambda e: e.activation(out=out, in_=in_, func=AF.Copy), r, w)
        else:
            self.p.op(eng, lambda e: e.tensor_copy(out=out, in_=in_), r, w)

    def mm1(self, out, lhsT, rhs, r, w, start=True, stop=True):
        self.p.mm([lambda e: e.matmul(out, lhsT, rhs, start=start, stop=stop)], r, w)

    def sb(self, es, name, shape, dt):
        self._uid = getattr(self, "_uid", 0) + 1
        nm = "sb%d_%s" % (self._uid, name)
        return Buf(es.enter_context(self.nc.sbuf_tensor(nm, shape, dt)), nm)

    def ring(self, es, name, shape, dt, n):
        return Ring([self.sb(es, "%s%d" % (name, i), shape, dt) for i in range(n)])

    def dram(self, name, shape, dt):
        kind = "ExternalOutput" if (self.debug and name in self.debug) else "Internal"
        if name in self.feed:
            kind = "ExternalInput"
        t = self.nc.dram_tensor(name, shape, dt, kind=kind).ap()
        return Buf(t, name)

    def tv(self, buf, ti, kc, w, k0=0, k1=None):
        k1 = kc if k1 is None else k1
        return buf[ti][:, k0 * 512:k1 * 512].rearrange("p (k t) -> p k t", t=512)[:, :, 0:w]

    def C(self, name, rows=slice(0, 128), cols=slice(0, 128)):
        o = self.coff[name]
        return self.cst[:, o:o + 128][rows, cols]

    def prm(self, l, name, j=0, n=1):
        o = self.poff[(l, name)] + j
        return self.PRM[:, o:o + n]

    def build(self):
        nc, p = self.nc, self.p
        def ein(name, shape):
            if self.lite and name in ("ada_w", "w_in", "w_out", "ffn_w_up", "ffn_w_down"):
                shape = [DEPTH, 128, 128]
            return nc.dram_tensor(name, shape, F32, kind="ExternalInput").ap()
        self.xT_in = ein("xT", [D, TOK])
        self.cT_in = ein("cT", [128, KC, 2])
        self.ada_w = ein("ada_w", [DEPTH, D, 6 * D])
        self.w_in = ein("w_in", [DEPTH, D, INC])
        self.w_out = ein("w_out", [DEPTH, D, D])
        self.w_up = ein("ffn_w_up", [DEPTH, D, 2 * FFN])
        self.w_down = ein("ffn_w_down", [DEPTH, FFN, D])
        self.prm_in = ein("prm", [128, self.nprm])
        self.lgw_in = ein("lgw", [DEPTH, 2, 2, 4, 128, 128])
        self.cst_in = ein("cst", [128, self.cst_arr.shape[1]])
        self.rope_in = ein("rope", [2, 128, NLAT])
        self.y_out = Buf(nc.dram_tensor("yT", [D, NLAT], F32, kind="ExternalOutput").ap(), "yT")
        self.xs = self.dram("xs", [D, TOK], F32)
        self.hT = self.dram("hT", [9, 128, KC * 512], BF16)
        self.gq = self.dram("gq", [4, 128, TOK], F32)
        self.gk = self.dram("gk", [4, 128, TOK], F32)
        self.gv = self.dram("gv", [4, 128, TOK], F32)
        self.gz = self.dram("gz", [4, 128, TOK], F32)
        self.gs = self.dram("gs", [16, TOK], F32)
        self.aq = self.dram("aq", [8, 128, TOK], BF16)
        self.ak = self.dram("ak", [2, 128, TOK], BF16)
        self.av = self.dram("av", [2, 128, TOK], BF16)
        self.lx = self.dram("lx", [4, 128, TOK], F32)
        self.lg = self.dram("lg", [4, 128, TOK], F32)
        self.yT = self.dram("ymix", [9, 128, KC * 512], BF16)
        self.of = [self.dram("of%d" % d, [4, 128, TOK], F32) for d in range(2)]
        self.actT = self.dram("actT", [9, 128, 43 * 512], BF16)

        with ExitStack() as gs:
            self.cstb = self.sb(gs, "cst", [128, self.cst_arr.shape[1]], F32)
            self.cst = self.cstb.t
            self.PRMb = self.sb(gs, "prm", [128, self.nprm], F32)
            self.PRM = self.PRMb.t
            self.cbf = self.sb(gs, "cbf", [128, 256], BF16)
            self.mod = self.sb(gs, "mod", [128, 96, 2], F32)
            self.ma = self.sb(gs, "ma", [128, 2, KC, 2], F32)
            self.scT = self.sb(gs, "scT", [128, KC, 2], F32)
            self.epsb = self.sb(gs, "epsb", [128, 1], F32)
            self.PS = [Buf(gs.enter_context(nc.psum_tensor("ps%d" % i, [128, 512], F32)), "ps%d" % i) for i in range(8)]
            p.dma("sp", self.cst[:, :], self.cst_in, w=[self.cstb.k()])
            p.dma("sp", self.PRM[:, :], self.prm_in, w=[self.PRMb.k()])
            p.dma("sp", self.scT[:, :, :], self.cT_in, w=[self.scT.k()])
            p.op("dve", lambda e: e.memset(self.epsb[:, :], EPS), w=[self.epsb.k()])
            self.cp("dve", self.cbf[:, 0:128], self.C("ident"), [self.cstb.k()], [self.cbf.k()])
            self.cp("dve", self.cbf[:, 128:256], self.C("ones"), [self.cstb.k()], [self.cbf.k()])
            self.act(self.scT[:, :, :], self.scT[:, :, :], AF.Silu, [self.scT.k()], [self.scT.k()])
            for ci in range(KC):
                p.dma("sp", self.xs[ci * 128:(ci + 1) * 128, :], self.xT_in[ci * 128:(ci + 1) * 128, :], w=[self.xs.k(ci)])
            p.barrier()
            stages = ["adaln", "norm1", "win", "attn", "lru", "gdn", "wout", "norm2", "ffnup", "ffndown"]
            if self.only:
                stages = list(self.only)
            def run_stage(fn, *a):
                with ExitStack() as es:
                    self.stage_es = es
                    fn(*a)
                    p.barrier()
                    p.emit()

            for l in range(self.n_layers):
                for st in stages:
                    run_stage(getattr(self, "st_" + st), l)
                    if self.stop_after == (l, st):
                        break
                else:
                    continue
                break
            else:
                run_stage(self.st_final)
        return self

    def st_adaln(self, l):
        nc, p = self.nc, self.p
        es = self.stage_es
        if True:
            wr = self.ring(es, "aw", [128, KC, 512], F32, 2)
            for pc in range(24):
                W = wr.get()
                p.dma("sp" if pc % 2 == 0 else "act", W[:, :, :],
                      self.ada_w[l][:, pc * 512:(pc + 1) * 512].rearrange("(k p) n -> p k n", p=128), w=[W.k()])
                for j in range(4):
                    ch = pc * 4 + j
                    P = self.PS[j % 2]
                    fns = [(lambda e, k=k, j=j, W=W, P=P: e.matmul(P[:, 0:2], W[:, k, j * 128:(j + 1) * 128], self.scT[:, k, :],
                                                                    start=(k == 0), stop=(k == KC - 1))) for k in range(KC)]
                    p.mm(fns, r=[W.k(), self.scT.k()], w=[P.k()])
                    self.ts("dve", self.mod[:, ch, :], P[:, 0:2], self.prm(l, "ada_b", ch), None, ALU.add, None,
                            [P.k(), self.PRMb.k()], [self.mod.k()])
            for i, (gname, base) in enumerate((("n1g", 16), ("n2g", 64))):
                for k in range(KC):
                    self.ts("dve", self.ma[:, i, k, :], self.mod[:, base + k, :], 1.0, self.prm(l, gname, k),
                            ALU.add, ALU.mult, [self.mod.k(), self.PRMb.k()], [self.ma.k()])

    def st_norm1(self, l):
        self._norm(l, 0, 0)

    def st_norm2(self, l):
        self._norm(l, 1, 48)

    def _norm(self, l, which, shbase):
        p = self.p
        es = self.stage_es
        if True:
            xr = self.ring(es, "nx", [128, KC, 512], F32, 2)
            sq = self.sb(es, "nsq", [128, KC, 512], F32)
            hr = self.ring(es, "nh", [128, KC, 512], BF16, 2)
            ss = self.sb(es, "nss", [128, 512], F32)
            rs = self.sb(es, "nrs", [128, 512], F32)
            for ti, (t0, w) in enumerate(TILES):
                col = 0 if t0 < NLAT else 1
                X = xr.get()
                H = hr.get()
                p.dma("sp", X[:, :, 0:w], self.xs[:, t0:t0 + w].rearrange("(k p) t -> p k t", p=128),
                      r=[self.xs.k(ci) for ci in range(KC)], w=[X.k()])
                self.act(sq[:, :, 0:w], X[:, :, 0:w], AF.Square, [X.k()], [sq.k()])
                p.op("dve", lambda e: e.tensor_reduce(out=ss[:, 0:w], in_=sq[:, :, 0:w].rearrange("p k t -> p t k"),
                                                      axis=AX.X, op=ALU.add), [sq.k()], [ss.k()])
                P = self.PS[ti % 2]
                self.mm1(P[:, 0:w], self.C("ones"), ss[:, 0:w], [ss.k(), self.cstb.k()], [P.k()])
                self.act(rs[:, 0:w], P[:, 0:w], AF.Ln, [P.k(), self.epsb.k()], [rs.k()], scale=1.0 / D, bias=self.epsb[:, 0:1])
                self.act(rs[:, 0:w], rs[:, 0:w], AF.Exp, [rs.k()], [rs.k()], scale=-0.5)
                self.tt("dve", sq[:, :, 0:w], X[:, :, 0:w], rs[:, 0:w].unsqueeze(1).to_broadcast([128, KC, w]), ALU.mult,
                        [X.k(), rs.k()], [sq.k()])
                for k in range(KC):
                    self.act(H[:, k, 0:w], sq[:, k, 0:w], AF.Identity, [sq.k(), self.ma.k(), self.mod.k()], [H.k()],
                             scale=self.ma[:, which, k, col:col + 1], bias=self.mod[:, shbase + k, col:col + 1])
                p.dma("pool", self.tv(self.hT, ti, KC, w), H[:, :, 0:w], r=[H.k()], w=[self.hT.k(ti)])

    def gemm(self, es, in_dram, kc, groups, wsrc, evac, tiles=TILES, in_ring=None, w_ring=None, pre_tile=None, post_group=None):
        p = self.p
        it = 0
        pend = [None]

        def advance(k):
            for _ in range(k):
                if pend[0] is None:
                    return
                try:
                    next(pend[0])
                except StopIteration:
                    pend[0] = None

        def load_w(grp):
            Ws = []
            for j, pc in enumerate(grp):
                W = w_ring.get()
                src, ncols = wsrc(pc)
                srcv = src.rearrange("(k p) n -> p k n", p=128)
                for k0 in range(0, kc, 16):
                    k1 = min(kc, k0 + 16)
                    p.dma("pool", W[:, k0:k1, 0:ncols], srcv[:, k0:k1, :], w=[W.k()])
                Ws.append((W, ncols))
            return Ws

        nextWs = load_w(groups[0])
        for gi, grp in enumerate(groups):
            Ws = nextWs
            for ti, (t0, w) in enumerate(tiles):
                A = in_ring.get()
                for k0 in range(0, kc, 16):
                    k1 = min(kc, k0 + 16)
                    p.dma("sp", A[:, k0:k1, 0:w], self.tv(in_dram, ti, kc, w, k0, k1), r=[in_dram.k(ti)], w=[A.k()])
                if pre_tile is not None:
                    pre_tile(gi, grp, ti, t0, w)
                for j, pc in enumerate(grp):
                    W, ncols = Ws[j]
                    P = self.PS[it % 4]
                    it += 1
                    fns = [(lambda e, k=k, W=W, A=A, P=P, ncols=ncols, w=w: e.matmul(
                        P[0:ncols, 0:w], W[:, k, 0:ncols], A[:, k, 0:w], start=(k == 0), stop=(k == kc - 1))) for k in range(kc)]
                    p.mm(fns, r=[W.k(), A.k()], w=[P.k()])
                    evac(gi, j, pc, ti, t0, w, P)
                advance(4)
            if gi + 1 < len(groups):
                nextWs = load_w(groups[gi + 1])
            advance(1 << 30)
            if post_group is not None:
                r_ = post_group(gi, grp)
                pend[0] = r_ if hasattr(r_, "__next__") else None
                advance(1 << 30)
        advance(1 << 30)

    def pnorm(self, src_ap, src_keys, w, scale, sqb, rsb, Pt):
        p = self.p
        self.act(sqb[:, 0:w], src_ap, AF.Square, src_keys, [sqb.k()])
        self.mm1(Pt[:, 0:w], self.C("ones"), sqb[:, 0:w], [sqb.k(), self.cstb.k()], [Pt.k()])
        self.act(rsb[:, 0:w], Pt[:, 0:w], AF.Ln, [Pt.k(), self.epsb.k()], [rsb.k()], scale=scale, bias=self.epsb[:, 0:1])
        self.act(rsb[:, 0:w], rsb[:, 0:w], AF.Exp, [rsb.k()], [rsb.k()], scale=-0.5)

    def conv4(self, out, U, wcol, bias=None):
        p = self.p
        n = UW - 4
        if bias is None:
            self.ts("dve", out[:, 2:2 + n], U[:, 2:2 + n], wcol(2), None, ALU.mult, None, [U.k(), self.PRMb.k()], [out.k()])
        else:
            self.ts("dve", out[:, 2:2 + n], U[:, 2:2 + n], wcol(2), bias, ALU.mult, ALU.add, [U.k(), self.PRMb.k()], [out.k()])
        for j, sh in ((0, -2), (1, -1), (3, 1)):
            self.stt(out[:, 2:2 + n], U[:, 2 + sh:2 + sh + n], wcol(j), out[:, 2:2 + n], ALU.mult, ALU.add,
                     [U.k(), out.k(), self.PRMb.k()], [out.k()])

    def st_win(self, l):
        p, nc = self.p, self.nc
        es = self.stage_es
        if True:
            in_ring = self.ring(es, "wa", [128, KC, 512], BF16, 2)
            w_ring = self.ring(es, "ww", [128, KC, 128], BF16, 8)
            U = [self.sb(es, "wu%d" % i, [128, UW], F32) for i in range(4)]
            T1 = self.sb(es, "wt1", [128, UW], F32)
            T2 = self.sb(es, "wt2", [128, UW], F32)
            sqb = self.sb(es, "wsq", [128, 512], F32)
            rsb = self.sb(es, "wrs", [128, 512], F32)
            t3r = self.ring(es, "wt3", [128, 512], F32, 2)
            t4r = self.ring(es, "wt4", [128, 512], F32, 2)
            obr = self.ring(es, "wob", [128, 512], BF16, 2)
            rope = self.ring(es, "wrp", [128, 2, 512], F32, 2)
            for u in U + [T1, T2]:
                p.op("pool", lambda e, u=u: e.memset(u[:, :], 0.0), w=[u.k()])
            groups = [[i, i + 1, i + 2, i + 3] for i in range(0, 36, 4)] + [[36]]
            coloff = ([i * 128 for i in range(16)] + [2064 + i * 128 for i in range(8)] +
                      [3088, 3216, 3344, 3472] + [3600 + i * 128 for i in range(4)] + [4112 + i * 128 for i in range(4)] + [2048])

            def wsrc(pc):
                n = 16 if pc == 36 else 128
                return self.w_in[l][:, coloff[pc]:coloff[pc] + n], n

            cnt = [0]

            def evac(gi, j, pc, ti, t0, w, P):
                n = 16 if pc == 36 else 128
                u = U[j]
                c0 = ucol(t0)
                if cnt[0] % 2 == 0:
                    self.cp("act", u[0:n, c0:c0 + w], P[0:n, 0:w], [P.k()], [u.k()])
                else:
                    self.cp("dve", u[0:n, c0:c0 + w], P[0:n, 0:w], [P.k()], [u.k()])
                cnt[0] += 1

            def store_seq(dst, idx, src, eng="sp"):
                p.dma(eng, dst[idx][:, 0:NLAT], src[:, 2:2 + NLAT], r=[src.k()], w=[dst.k(idx, 0)])
                p.dma(eng, dst[idx][:, NLAT:TOK], src[:, NLAT + 6:NLAT + 6 + NCTX], r=[src.k()], w=[dst.k(idx, 1)])

            def post_group(gi, grp):
                for j, pc in enumerate(grp):
                    u = U[j]
                    if pc < 12:
                        kind, h = pc // 4, pc % 4
                        self.conv4(T1, u, lambda tap, pc=pc: self.prm(l, "gconv", pc * 4 + tap))
                        self.act(T2[:, :], T1[:, :], AF.Silu, [T1.k()], [T2.k()])
                        yield
                        if kind == 2:
                            store_seq(self.gv, h, T2)
                            yield
                        else:
                            for ti, (t0, w) in enumerate(TILES):
                                c0 = ucol(t0)
                                Pt = self.PS[4 + ti % 2]
                                self.pnorm(T2[:, c0:c0 + w], [T2.k()], w, 1.0, sqb, rsb, Pt)
                                t3 = t3r.get()
                                if kind == 0:
                                    self.stt(t3[:, 0:w], T2[:, c0:c0 + w], HD ** -0.5, rsb[:, 0:w], ALU.mult, ALU.mult,
                                             [T2.k(), rsb.k()], [t3.k()])
                                else:
                                    self.tt("dve", t3[:, 0:w], T2[:, c0:c0 + w], rsb[:, 0:w], ALU.mult, [T2.k(), rsb.k()], [t3.k()])
                                dst = self.gq if kind == 0 else self.gk
                                p.dma("sp", dst[h][:, t0:t0 + w], t3[:, 0:w], r=[t3.k()], w=[dst.k(h, ti)])
                                yield
                    elif pc < 16:
                        self.act(T2[:, :], u[:, :], AF.Silu, [u.k()], [T2.k()])
                        store_seq(self.gz, pc - 12, T2)
                        yield
                    elif pc < 26:
                        isq = pc < 24
                        h = pc - 16 if isq else pc - 24
                        dst = self.aq if isq else self.ak
                        gcol = self.prm(l, "qng" if isq else "kng")
                        for ti, (t0, w) in enumerate(TILES):
                            c0 = ucol(t0)
                            Pt = self.PS[4 + ti % 2]
                            self.pnorm(u[:, c0:c0 + w], [u.k()], w, 1.0 / HD, sqb, rsb, Pt)
                            t3 = t3r.get()
                            ob = obr.get()
                            self.stt(t3[:, 0:w], u[:, c0:c0 + w], gcol, rsb[:, 0:w], ALU.mult, ALU.mult,
                                     [u.k(), rsb.k(), self.PRMb.k()], [t3.k()])
                            if t0 < NLAT:
                                rp = rope.get()
                                p.dma("act", rp[:, :, 0:w], self.rope_in[:, :, t0:t0 + w].rearrange("a p t -> p a t"), w=[rp.k()])
                                Pr = self.PS[6 + ti % 2]
                                self.mm1(Pr[:, 0:w], self.C("perm"), t3[:, 0:w], [t3.k(), self.cstb.k()], [Pr.k()])
                                t4 = t4r.get()
                                self.tt("dve", t4[:, 0:w], Pr[:, 0:w], rp[:, 1, 0:w], ALU.mult, [Pr.k(), rp.k()], [t4.k()])
                                self.tt("pool", t3[:, 0:w], t3[:, 0:w], rp[:, 0, 0:w], ALU.mult, [t3.k(), rp.k()], [t3.k()])
                                self.tt("dve", ob[:, 0:w], t3[:, 0:w], t4[:, 0:w], ALU.add, [t3.k(), t4.k()], [ob.k()])
                            else:
                                self.cp("dve", ob[:, 0:w], t3[:, 0:w], [t3.k()], [ob.k()])
                            p.dma("sp", dst[h][:, t0:t0 + w], ob[:, 0:w], r=[ob.k()], w=[dst.k(h, ti)])
                            yield
                    elif pc < 28:
                        h = pc - 26
                        for ti, (t0, w) in enumerate(TILES):
                            c0 = ucol(t0)
                            ob = obr.get()
                            self.cp("act", ob[:, 0:w], u[:, c0:c0 + w], [u.k()], [ob.k()])
                            p.dma("sp", self.av[h][:, t0:t0 + w], ob[:, 0:w], r=[ob.k()], w=[self.av.k(h, ti)])
                            yield
                    elif pc < 32:
                        h = pc - 28
                        self.conv4(T1, u, lambda tap, h=h: self.prm(l, "lconv", h * 4 + tap), bias=self.prm(l, "lcb", h))
                        store_seq(self.lx, h, T1)
                        yield
                    elif pc < 36:
                        h = pc - 32
                        self.act(T2[:, :], u[:, :], AF.Gelu, [u.k()], [T2.k()])
                        store_seq(self.lg, h, T2)
                        yield
                    else:
                        p.dma("sp", self.gs[:, 0:NLAT], u[0:16, 2:2 + NLAT], r=[u.k()], w=[self.gs.k(0)])
                        p.dma("sp", self.gs[:, NLAT:TOK], u[0:16, NLAT + 6:NLAT + 6 + NCTX], r=[u.k()], w=[self.gs.k(1)])
                        yield

            self.gemm(es, self.hT, KC, groups, wsrc, evac, in_ring=in_ring, w_ring=w_ring, post_group=post_group)

    def st_attn(self, l):
        p = self.p
        scale = HD ** -0.5
        es = self.stage_es
        if True:
            KT = self.sb(es, "akt", [128, TOK], BF16)
            VT = self.sb(es, "avt", [128, TOK], BF16)
            VK = self.sb(es, "avk", [128, NCH, 128], BF16)
            qr = self.ring(es, "aqt", [128, 512], BF16, 2)
            ptr = self.ring(es, "apt", [128, 512], BF16, 4)
            rd = self.sb(es, "ard", [128, 512], F32)
            yr = self.ring(es, "ayo", [128, 512], BF16, 2)
            identb = self.cbf[:, 0:128]
            onesb = self.cbf[:, 128:256]
            for kv in range(2):
                p.dma("sp", KT[:, :], self.ak[kv], r=[self.ak.k(kv, ti) for ti in range(9)], w=[KT.k()])
                p.dma("sp", VT[:, :], self.av[kv], r=[self.av.k(kv, ti) for ti in range(9)], w=[VT.k()])
                for c in range(NCH):
                    P = self.PS[c % 2]
                    self.mm1(P[:, 0:128], VT[:, c * 128:(c + 1) * 128], identb, [VT.k(), self.cbf.k()], [P.k()])
                    self.cp("act" if c % 2 == 0 else "dve", VK[:, c, :], P[:, 0:128], [P.k()], [VK.k()])
                it = 0
                for hq in range(4):
                    h = kv * 4 + hq
                    for ti, (t0, w) in enumerate(TILES):
                        Q = qr.get()
                        p.dma("sp", Q[:, 0:w], self.aq[h][:, t0:t0 + w], r=[self.aq.k(h, ti)], w=[Q.k()])
                        chunks = list(range(NCH)) if t0 < NLAT else [32, 33]
                        PO = self.PS[4 + it % 2]
                        PD = self.PS[6 + it % 2]
                        it += 1
                        LOOK = 2
                        pts = {}
                        nch = len(chunks)
                        for ci in range(nch + LOOK):
                            if ci < nch:
                                c = chunks[ci]
                                Psc = self.PS[ci % 4]
                                self.mm1(Psc[:, 0:w], KT[:, c * 128:(c + 1) * 128], Q[:, 0:w], [KT.k(), Q.k()], [Psc.k()])
                                PT = ptr.get()
                                self.act(PT[:, 0:w], Psc[:, 0:w], AF.Exp, [Psc.k()], [PT.k()], scale=scale)
                                pts[ci] = PT
                            cj = ci - LOOK
                            if cj >= 0:
                                c = chunks[cj]
                                PT = pts.pop(cj)
                                first, last = cj == 0, cj == nch - 1
                                p.mm([lambda e, c=c, PT=PT, PO=PO, first=first, last=last, w=w: e.matmul(
                                         PO[:, 0:w], VK[:, c, :], PT[:, 0:w], start=first, stop=last),
                                      lambda e, PT=PT, PD=PD, first=first, last=last, w=w: e.matmul(
                                         PD[:, 0:w], onesb, PT[:, 0:w], start=first, stop=last)],
                                     r=[VK.k(), PT.k(), self.cbf.k()], w=[PO.k(), PD.k()])
                        p.op("dve", lambda e, PD=PD, w=w: e.reciprocal(out=rd[:, 0:w], in_=PD[:, 0:w]), [PD.k()], [rd.k()])
                        Y = yr.get()
                        self.tt("dve", Y[:, 0:w], PO[:, 0:w], rd[:, 0:w], ALU.mult, [PO.k(), rd.k()], [Y.k()])
                        p.dma("sp", self.yT[ti][:, (4 + h) * 512:(4 + h) * 512 + w], Y[:, 0:w], r=[Y.k()], w=[self.yT.k(4 + h, ti)])

    def st_lru(self, l):
        p = self.p
        es = self.stage_es
        if True:
            X = self.sb(es, "lX", [128, TOK], F32)
            G = self.sb(es, "lG", [128, TOK], F32)
            HF = self.sb(es, "lHF", [128, TOK], F32)
            HB = self.sb(es, "lHB", [128, TOK], F32)
            Wg = self.sb(es, "lW", [128, 4, 128], F32)
            c1 = self.sb(es, "lc1", [128, 4], F32)
            tr = {n: self.ring(es, "l" + n, [128, 512], F32, 2) for n in ("r", "i", "a", "s", "u")}
            yr = self.ring(es, "ly", [128, 512], BF16, 2)
            for pc in range(4):
                p.dma("sp", X[:, :], self.lx[pc], r=[self.lx.k(pc, 0), self.lx.k(pc, 1)], w=[X.k()])
                p.dma("act", G[:, :], self.lg[pc], r=[self.lg.k(pc, 0), self.lg.k(pc, 1)], w=[G.k()])
                for d in range(2):
                    for g in range(2):
                        p.dma("sp", Wg[:, d * 2 + g, :], self.lgw_in[l, d, g, pc], w=[Wg.k()])
                    lam = self.prm(l, "llam", d * 4 + pc)
                    self.act(c1[:, 2 * d:2 * d + 1], lam, AF.Exp, [self.PRMb.k()], [c1.k()], scale=-1.0)
                    self.act(c1[:, 2 * d:2 * d + 1], c1[:, 2 * d:2 * d + 1], AF.Ln, [c1.k()], [c1.k()], bias=1.0)
                    self.ts("dve", c1[:, 2 * d + 1:2 * d + 2], c1[:, 2 * d:2 * d + 1], -16.0, None, ALU.mult, None, [c1.k()], [c1.k()])
                    self.ts("dve", c1[:, 2 * d:2 * d + 1], c1[:, 2 * d:2 * d + 1], -8.0, None, ALU.mult, None, [c1.k()], [c1.k()])
                for d in range(2):
                    order = [8] + (list(range(8)) if d == 0 else list(range(7, -1, -1)))
                    Hd = HF if d == 0 else HB
                    prev = None
                    for ti in order:
                        t0, w = TILES[ti]
                        Pr = self.PS[0 + ti % 2]
                        Pi = self.PS[2 + ti % 2]
                        self.mm1(Pr[:, 0:w], Wg[:, d * 2 + 0, :], X[:, t0:t0 + w], [Wg.k(), X.k()], [Pr.k()])
                        self.mm1(Pi[:, 0:w], Wg[:, d * 2 + 1, :], X[:, t0:t0 + w], [Wg.k(), X.k()], [Pi.k()])
                        R, I_, A, S, Uu = (tr[n].get() for n in ("r", "i", "a", "s", "u"))
                        self.act(R[:, 0:w], Pr[:, 0:w], AF.Sigmoid, [Pr.k(), self.PRMb.k()], [R.k()], bias=self.prm(l, "lgb", (d * 2 + 0) * 4 + pc))
                        self.act(I_[:, 0:w], Pi[:, 0:w], AF.Sigmoid, [Pi.k(), self.PRMb.k()], [I_.k()], bias=self.prm(l, "lgb", (d * 2 + 1) * 4 + pc))
                        self.act(A[:, 0:w], R[:, 0:w], AF.Exp, [R.k(), c1.k()], [A.k()], scale=c1[:, 2 * d:2 * d + 1])
                        self.act(S[:, 0:w], R[:, 0:w], AF.Exp, [R.k(), c1.k()], [S.k()], scale=c1[:, 2 * d + 1:2 * d + 2])
                        self.act(S[:, 0:w], S[:, 0:w], AF.Sqrt, [S.k()], [S.k()], scale=-1.0, bias=1.0)
                        self.tt("pool", Uu[:, 0:w], I_[:, 0:w], X[:, t0:t0 + w], ALU.mult, [I_.k(), X.k()], [Uu.k()])
                        self.tt("dve", Uu[:, 0:w], Uu[:, 0:w], S[:, 0:w], ALU.mult, [Uu.k(), S.k()], [Uu.k()])
                        if prev is None:
                            init = 0.0
                        else:
                            pt0, pw = TILES[prev]
                            init = Hd[:, pt0 + pw - 1:pt0 + pw] if d == 0 else Hd[:, pt0:pt0 + 1]
                        if d == 0:
                            p.op("dve", lambda e, A=A, Uu=Uu, init=init, t0=t0, w=w: e.tensor_tensor_scan(
                                out=Hd[:, t0:t0 + w], data0=A[:, 0:w], data1=Uu[:, 0:w], initial=init, op0=ALU.mult, op1=ALU.add),
                                 [A.k(), Uu.k(), Hd.k(prev)], [Hd.k(ti)])
                        else:
                            p.op("dve", lambda e, A=A, Uu=Uu, init=init, t0=t0, w=w: e.tensor_tensor_scan(
                                out=Hd[:, t0:t0 + w][:, ::-1], data0=A[:, 0:w][:, ::-1], data1=Uu[:, 0:w][:, ::-1], initial=init,
                                op0=ALU.mult, op1=ALU.add), [A.k(), Uu.k(), Hd.k(prev)], [Hd.k(ti)])
                        prev = ti
                for ti, (t0, w) in enumerate(TILES):
                    Y = yr.get()
                    S = tr["s"].get()
                    self.tt("pool", S[:, 0:w], HF[:, t0:t0 + w], HB[:, t0:t0 + w], ALU.add, [HF.k(ti), HB.k(ti)], [S.k()])
                    self.tt("dve", Y[:, 0:w], S[:, 0:w], G[:, t0:t0 + w], ALU.mult, [S.k(), G.k()], [Y.k()])
                    p.dma("sp", self.yT[ti][:, (12 + pc) * 512:(12 + pc) * 512 + w], Y[:, 0:w], r=[Y.k()], w=[self.yT.k(12 + pc, ti)])

    def st_gdn(self, l):
        p = self.p
        es = self.stage_es
        if True:
            GS = self.sb(es, "gGS", [16, TOK], F32)
            SM = self.sb(es, "gSM", [128, NCH, 16], F32)
            BT = self.sb(es, "gBT", [128, NCH, 8], F32)
            NB = self.sb(es, "gNB", [128, NCH, 8], F32)
            LA = self.sb(es, "gLA", [128, NCH, 8], F32)
            GT = self.sb(es, "gGT", [128, NCH, 8], F32)
            EG = self.sb(es, "gEG", [128, NCH, 8], F32)
            EX = self.sb(es, "gEX", [128, NCH, 8], F32)
            NBEG = self.sb(es, "gNBEG", [128, NCH, 8], F32)
            nea = self.sb(es, "gnea", [128, 8], F32)
            S = [self.sb(es, "gS%d" % i, [128, 128], F32) for i in range(8)]
            Sb = [self.sb(es, "gSb%d" % i, [128, 128], BF16) for i in range(8)]
            def mk_rings(tag):
                R = lambda name, dt=F32, n=3: self.ring(es, "g%s%d" % (name, tag), [128, 128], dt, n)
                return (R("q", n=2), R("k", n=2), R("v", n=2), R("T1", n=3), R("D", n=2), R("DS", n=2), R("DT", n=2), R("EGr", n=2),
                        R("Qm", n=4), R("Pm", n=3), R("Rm", n=3), R("Po", n=2), R("vb", n=2), R("kd", n=2), R("at", n=2), R("qd", n=2),
                        R("r", n=3), R("vn", n=3), R("o", n=2))

            RS = [mk_rings(h) for h in range(4)]
            class _PB:
                def __init__(self, b):
                    self.b = b

                def __getitem__(self, idx):
                    return self.b[idx]

                def k(self, *a):
                    return self.b.k()

            pst = [(_PB(self.PS[i]), 0) for i in range(8)]
            psi = [0]

            def PT():
                b, o = pst[psi[0] % 8]
                psi[0] += 1
                return b, o

            p.dma("sp", GS[:, :], self.gs[:, :], r=[self.gs.k(0), self.gs.k(1)], w=[GS.k()])
            for n in range(NCH):
                P = self.PS[n % 2]
                self.mm1(P[:, 0:16], GS[0:16, n * 128:(n + 1) * 128], self.C("ident", slice(0, 16), slice(0, 16)),
                         [GS.k(), self.cstb.k()], [P.k()])
                self.cp("act" if n % 2 == 0 else "dve", SM[:, n, :], P[:, 0:16], [P.k()], [SM.k()])
            self.act(BT[:, :, :], SM[:, :, 0:8], AF.Sigmoid, [SM.k()], [BT.k()])
            self.ts("dve", NB[:, :, :], BT[:, :, :], -1.0, None, ALU.mult, None, [BT.k()], [NB.k()])
            self.act(nea[:, :], self.prm(l, "alog", 0, 8), AF.Exp, [self.PRMb.k()], [nea.k()])
            self.ts("dve", nea[:, :], nea[:, :], -1.0, None, ALU.mult, None, [nea.k()], [nea.k()])
            self.tt("dve", LA[:, :, :], SM[:, :, 8:16], self.prm(l, "dtb", 0, 8).unsqueeze(1).to_broadcast([128, NCH, 8]), ALU.add,
                    [SM.k(), self.PRMb.k()], [LA.k()])
            self.act(LA[:, :, :], LA[:, :, :], AF.Exp, [LA.k()], [LA.k()])
            self.act(LA[:, :, :], LA[:, :, :], AF.Ln, [LA.k()], [LA.k()], bias=1.0)
            self.tt("dve", LA[:, :, :], LA[:, :, :], nea[:, :].unsqueeze(1).to_broadcast([128, NCH, 8]), ALU.mult, [LA.k(), nea.k()], [LA.k()])
            if self.debug and "dbgLA" in self.debug:
                dl = self.dram("dbgLA", [128, NCH * 8], F32)
                p.dma("sp", dl[:, :], LA[:, :, :].rearrange("p n j -> p (n j)"), r=[LA.k()], w=[dl.k()])
            for n in range(NCH):
                P = self.PS[n % 2]
                fns = []
                for d in range(2):
                    inct = self.C("inctF" if d == 0 else "inctB")
                    strm = self.C("strF" if d == 0 else "strB")
                    fns.append(lambda e, d=d, n=n, P=P, inct=inct: e.matmul(P[:, d * 4:d * 4 + 4], inct, LA[:, n, d * 4:d * 4 + 4], start=True, stop=True))
                    fns.append(lambda e, d=d, n=n, P=P, strm=strm: e.matmul(P[:, 8 + d * 4:8 + d * 4 + 4], strm, LA[:, n, d * 4:d * 4 + 4], start=True, stop=True))
                p.mm(fns, r=[LA.k(), self.cstb.k()], w=[P.k()])
                self.cp("dve", GT[:, n, :], P[:, 0:8], [P.k()], [GT.k()])
                self.ts("dve", EG[:, n, :], P[:, 0:8], -BIG, None, ALU.max, None, [P.k()], [EG.k()])
                self.ts("dve", EX[:, n, :], P[:, 8:16], -BIG, None, ALU.max, None, [P.k()], [EX.k()])
            self.act(EG[:, :, :], EG[:, :, :], AF.Exp, [EG.k()], [EG.k()])
            self.act(EX[:, :, :], EX[:, :, :], AF.Exp, [EX.k()], [EX.k()])
            self.tt("dve", NBEG[:, :, :], NB[:, :, :], EG[:, :, :], ALU.mult, [NB.k(), EG.k()], [NBEG.k()])
            for i in range(8):
                p.op("dve", lambda e, i=i: e.memset(S[i][:, :], 0.0), w=[S[i].k()])
            ident = self.C("ident")
            orderF = [32, 33] + list(range(32))
            orderB = [33, 32] + list(range(31, -1, -1))
            ev = [0]

            def evac(out, P, o, r, w):
                ev[0] += 1
                self.cp("act" if ev[0] % 2 == 0 else "dve", out, P[:, o:o + 128], r, w)

            def inst(step, d, h):
                n = (orderF if d == 0 else orderB)[step]
                cs = slice(n * 128, (n + 1) * 128)
                ti = min(n // 4, 8)
                sfx = "F" if d == 0 else "B"
                j = d * 4 + h
                (rq, rk, rv, rT1, rD, rDS, rDT, rEGr, rQ, rP, rR, rPo, rvb, rkd, rat, rqd, rr, rvn, ro) = RS[h]
                q, k, v = rq.get(), rk.get(), rv.get()
                p.dma("sp", q[:, :], self.gq[h][:, cs], r=[self.gq.k(h, ti)], w=[q.k()])
                p.dma("act", k[:, :], self.gk[h][:, cs], r=[self.gk.k(h, ti)], w=[k.k()])
                p.dma("sp", v[:, :], self.gv[h][:, cs], r=[self.gv.k(h, 0), self.gv.k(h, 1)], w=[v.k()])
                qb, kb = q, k
                if self.gdn_level < 1.1:
                    return
                Pg, og = PT()
                self.mm1(Pg[:, og:og + 128], LA[:, n, j:j + 1].to_broadcast([128, 128]), self.C("inct" + sfx),
                         [LA.k(), self.cstb.k()], [Pg.k(og)])
                T1 = rT1.get(); Dm = rD.get(); DS = rDS.get(); DT = rDT.get(); EGr = rEGr.get()
                if self.gdn_level < 1.02:
                    return
                self.ts("dve", T1[:, :], Pg[:, og:og + 128], GT[:, n, j:j + 1], BIG, ALU.subtract, ALU.min,
                        [Pg.k(og), GT.k()], [T1.k()])
                self.tt("dve", T1[:, :], T1[:, :], self.C("neg" + sfx), ALU.max, [T1.k(), self.cstb.k()], [T1.k()])
                self.act(Dm[:, :], T1[:, :], AF.Exp, [T1.k()], [Dm.k()], scale=-1.0)
                T1b = rT1.get()
                self.ts("dve", T1b[:, :], Pg[:, og:og + 128], GT[:, n, j:j + 1], -BIG, ALU.subtract, ALU.max,
                        [Pg.k(og), GT.k()], [T1b.k()])
                self.tt("dve", T1b[:, :], T1b[:, :], self.C("negt" + sfx), ALU.min, [T1b.k(), self.cstb.k()], [T1b.k()])
                self.act(DT[:, :], T1b[:, :], AF.Exp, [T1b.k()], [DT.k()])
                T1c = rT1.get()
                self.ts("dve", T1c[:, :], Pg[:, og:og + 128], -BIG, None, ALU.max, None, [Pg.k(og)], [T1c.k()])
                self.act(EGr[:, :], T1c[:, :], AF.Exp, [T1c.k()], [EGr.k()])
                self.tt("dve", DS[:, :], Dm[:, :], self.C("offd"), ALU.mult, [Dm.k(), self.cstb.k()], [DS.k()])
                yield
                if self.gdn_level < 1.2:
                    return
                Pk, ok = PT()
                self.mm1(Pk[:, ok:ok + 128], kb[:, :], kb[:, :], [kb.k()], [Pk.k(ok)])
                Q0 = rQ.get()
                self.stt(Q0[:, :], Pk[:, ok:ok + 128], NB[:, n, j:j + 1], DS[:, :], ALU.mult, ALU.mult,
                         [Pk.k(ok), NB.k(), DS.k()], [Q0.k()])
                Pq, oq = PT()
                self.mm1(Pq[:, oq:oq + 128], kb[:, :], qb[:, :], [kb.k(), qb.k()], [Pq.k(oq)])
                at = rat.get()
                self.tt("dve", at[:, :], Pq[:, oq:oq + 128], DT[:, :], ALU.mult, [Pq.k(oq), DT.k()], [at.k()])
                qd = rqd.get()
                self.tt("dve", qd[:, :], q[:, :], EGr[:, :], ALU.mult, [q.k(), EGr.k()], [qd.k()])
                yield
                if self.gdn_level < 1.3:
                    return
                Pa, oa = PT()
                self.mm1(Pa[:, oa:oa + 128], k[:, :], ident, [k.k(), self.cstb.k()], [Pa.k(oa)])
                kd = rkd.get()
                self.act(kd[:, :], Pa[:, oa:oa + 128], AF.Identity, [Pa.k(oa), EX.k()], [kd.k()], scale=EX[:, n, j:j + 1])
                Pv, ov = PT()
                self.mm1(Pv[:, ov:ov + 128], v[:, :], ident, [v.k(), self.cstb.k()], [Pv.k(ov)])
                vb = rvb.get()
                self.act(vb[:, :], Pv[:, ov:ov + 128], AF.Identity, [Pv.k(ov), BT.k()], [vb.k()], scale=BT[:, n, j:j + 1])
                yield
                if self.gdn_level < 1.4:
                    return
                Pp, op_ = PT()
                self.mm1(Pp[:, op_:op_ + 128], Q0[:, :], ident, [Q0.k(), self.cstb.k()], [Pp.k(op_)])
                Pm = rP.get()
                self.tt("dve", Pm[:, :], Pp[:, op_:op_ + 128], self.C("bd"), ALU.mult, [Pp.k(op_), self.cstb.k()], [Pm.k()])
                Poff = rPo.get()
                self.tt("dve", Poff[:, :], Pp[:, op_:op_ + 128], self.C("offb"), ALU.mult, [Pp.k(op_), self.cstb.k()], [Poff.k()])
                Qb = rQ.get()
                self.tt("dve", Qb[:, :], Q0[:, :], self.C("bd"), ALU.mult, [Q0.k(), self.cstb.k()], [Qb.k()])
                Rm = rR.get()
                self.tt("dve", Rm[:, :], Pm[:, :], ident, ALU.add, [Pm.k(), self.cstb.k()], [Rm.k()])
                yield
                Qm = Qb
                if self.gdn_level < 2:
                    return
                for it in range(1, 6):
                    Qn = rQ.get()
                    Pa2, oa2 = PT()
                    self.mm1(Pa2[:, oa2:oa2 + 128], Pm[:, :], Qm[:, :], [Pm.k(), Qm.k()], [Pa2.k(oa2)])
                    evac(Qn[:, :], Pa2, oa2, [Pa2.k(oa2)], [Qn.k()])
                    yield
                    if it < 5:
                        Pn = rP.get()
                        Pb2, ob2 = PT()
                        self.mm1(Pb2[:, ob2:ob2 + 128], Qm[:, :], Pm[:, :], [Pm.k(), Qm.k()], [Pb2.k(ob2)])
                        evac(Pn[:, :], Pb2, ob2, [Pb2.k(ob2)], [Pn.k()])
                        yield
                    Pc2, oc2 = PT()
                    self.mm1(Pc2[:, oc2:oc2 + 128], Qn[:, :], Rm[:, :], [Qn.k(), Rm.k()], [Pc2.k(oc2)])
                    Rn = rR.get()
                    self.tt("dve", Rn[:, :], Pc2[:, oc2:oc2 + 128], Rm[:, :], ALU.add, [Pc2.k(oc2), Rm.k()], [Rn.k()])
                    yield
                    Qm, Rm = Qn, Rn
                    if it < 5:
                        Pm = Pn
                TT = Rm
                if self.gdn_level < 3:
                    return
                Sj, Sbj = S[j], S[j]
                P1, o1 = PT()
                self.mm1(P1[:, o1:o1 + 128], kb[:, :], Sbj[:, :], [kb.k(), Sbj.k()], [P1.k(o1)])
                rr_ = rr.get()
                self.stt(rr_[:, :], P1[:, o1:o1 + 128], NBEG[:, n, j:j + 1], vb[:, :], ALU.mult, ALU.add,
                         [P1.k(o1), NBEG.k(), vb.k()], [rr_.k()])
                P2a, o2a = PT()
                self.mm1(P2a[:, o2a:o2a + 128], TT[:, :], rr_[:, :], [TT.k(), rr_.k()], [P2a.k(o2a)])
                v1 = rvn.get()
                self.cp("act", v1[:, :], P2a[:, o2a:o2a + 128], [P2a.k(o2a)], [v1.k()])
                yield
                P2b, o2b = PT()
                self.mm1(P2b[:, o2b:o2b + 128], Poff[:, :], v1[:, :], [Poff.k(), v1.k()], [P2b.k(o2b)])
                r2 = rr.get()
                self.tt("dve", r2[:, :], P2b[:, o2b:o2b + 128], rr_[:, :], ALU.add, [P2b.k(o2b), rr_.k()], [r2.k()])
                yield
                P2, o2 = PT()
                self.mm1(P2[:, o2:o2 + 128], TT[:, :], r2[:, :], [TT.k(), r2.k()], [P2.k(o2)])
                vn = rvn.get()
                self.cp("act", vn[:, :], P2[:, o2:o2 + 128], [P2.k(o2)], [vn.k()])
                yield
                P3, o3 = PT()
                p.mm([lambda e, P3=P3, o3=o3, Sbj=Sbj, qd=qd: e.matmul(P3[:, o3:o3 + 128], Sbj[:, :], qd[:, :], start=True, stop=False),
                      lambda e, P3=P3, o3=o3, vn=vn, at=at: e.matmul(P3[:, o3:o3 + 128], vn[:, :], at[:, :], start=False, stop=True)],
                     r=[Sbj.k(), qd.k(), vn.k(), at.k()], w=[P3.k(o3)])
                o_ = ro.get()
                self.cp("dve", o_[:, :], P3[:, o3:o3 + 128], [P3.k(o3)], [o_.k()])
                p.dma("sp", self.of[d][h][:, cs], o_[:, :], r=[o_.k()], w=[self.of[d].k(h, n)])
                yield
                P4, o4 = PT()
                self.mm1(P4[:, o4:o4 + 128], kd[:, :], vn[:, :], [kd.k(), vn.k()], [P4.k(o4)])
                endc = 127 if d == 0 else 0
                self.stt(Sj[:, :], Sj[:, :], EGr[:, endc:endc + 1], P4[:, o4:o4 + 128], ALU.mult, ALU.add,
                         [Sj.k(), EGr.k(), P4.k(o4)], [Sj.k()])

            for step in range(self.gdn_steps if self.gdn_level >= 1 else 0):
                for d in range(2):
                    gens = [inst(step, d, h) for h in range(4)]
                    while gens:
                        for g_ in list(gens):
                            try:
                                next(g_)
                            except StopIteration:
                                gens.remove(g_)
            p.barrier()
            fr = {n: self.ring(es, "gf" + n, [128, 512], F32, 2) for n in ("a", "b", "z", "q", "r")}
            yr = self.ring(es, "gfy", [128, 512], BF16, 2)
            for h in range(4):
                for ti, (t0, w) in enumerate(TILES):
                    A, B, Z, SQ, RS = (fr[n].get() for n in ("a", "b", "z", "q", "r"))
                    p.dma("sp", A[:, 0:w], self.of[0][h][:, t0:t0 + w], w=[A.k()])
                    p.dma("act", B[:, 0:w], self.of[1][h][:, t0:t0 + w], w=[B.k()])
                    p.dma("sp", Z[:, 0:w], self.gz[h][:, t0:t0 + w], w=[Z.k()])
                    self.tt("dve", A[:, 0:w], A[:, 0:w], B[:, 0:w], ALU.add, [A.k(), B.k()], [A.k()])
                    Pt = self.PS[ti % 2]
                    self.pnorm(A[:, 0:w], [A.k()], w, 1.0 / HD, SQ, RS, Pt)
                    self.stt(A[:, 0:w], A[:, 0:w], self.prm(l, "gng"), RS[:, 0:w], ALU.mult, ALU.mult,
                             [A.k(), RS.k(), self.PRMb.k()], [A.k()])
                    Y = yr.get()
                    self.tt("dve", Y[:, 0:w], A[:, 0:w], Z[:, 0:w], ALU.mult, [A.k(), Z.k()], [Y.k()])
                    p.dma("sp", self.yT[ti][:, h * 512:h * 512 + w], Y[:, 0:w], r=[Y.k()], w=[self.yT.k(h, ti)])

    def _resid_gemm(self, l, es, in_dram, kc, wfull, gbase, ngrp, in_ring, w_ring):
        p = self.p
        xr = self.ring(es, "rx", [128, 4, 512], F32, 2)
        cur = {}

        def wsrc(pc):
            return wfull[:, pc * 128:(pc + 1) * 128], 128

        def pre_tile(gi, grp, ti, t0, w):
            Xb = xr.get()
            g0 = grp[0]
            p.dma("act", Xb[:, 0:len(grp), 0:w], self.xs[g0 * 128:(g0 + len(grp)) * 128, t0:t0 + w].rearrange("(g p) t -> p g t", p=128),
                  r=[self.xs.k(pc) for pc in grp], w=[Xb.k()])
            cur["x"] = Xb

        def evac(gi, j, pc, ti, t0, w, P):
            Xb = cur["x"]
            col = 0 if t0 < NLAT else 1
            self.stt(Xb[:, j, 0:w], P[:, 0:w], self.mod[:, gbase + pc, col:col + 1], Xb[:, j, 0:w], ALU.mult, ALU.add,
                     [P.k(), self.mod.k(), Xb.k()], [Xb.k()])
            if j == len(cur["grp"]) - 1:
                grp = cur["grp"]
                g0 = grp[0]
                p.dma("act", self.xs[g0 * 128:(g0 + len(grp)) * 128, t0:t0 + w].rearrange("(g p) t -> p g t", p=128),
                      Xb[:, 0:len(grp), 0:w], r=[Xb.k()], w=[self.xs.k(pc) for pc in grp])

        groups = [list(range(g * 4, g * 4 + 4)) for g in range(ngrp)]

        def pre_tile2(gi, grp, ti, t0, w):
            cur["grp"] = grp
            pre_tile(gi, grp, ti, t0, w)

        self.gemm(es, in_dram, kc, groups, wsrc, evac, in_ring=in_ring, w_ring=w_ring, pre_tile=pre_tile2)

    def st_wout(self, l):
        es = self.stage_es
        if True:
            in_ring = self.ring(es, "oa", [128, KC, 512], BF16, 2)
            w_ring = self.ring(es, "ow", [128, KC, 128], BF16, 8)
            self._resid_gemm(l, es, self.yT, KC, self.w_out[l], 32, 4, in_ring, w_ring)

    def st_ffndown(self, l):
        es = self.stage_es
        if True:
            in_ring = self.ring(es, "da", [128, 43, 512], BF16, 2)
            w_ring = self.ring(es, "dw", [128, 43, 128], BF16, 6)
            self._resid_gemm(l, es, self.actT, 43, self.w_down[l], 80, 4, in_ring, w_ring)

    def st_ffnup(self, l):
        p = self.p
        es = self.stage_es
        HT = self.sb(es, "uHT", [128, KC, TOK], BF16)
        w_ring = self.ring(es, "uw", [128, KC, 128], BF16, 4)
        eg_r = self.ring(es, "ueg", [128, 514], F32, 3)
        eu_r = self.ring(es, "ueu", [128, 514], F32, 3)
        tg_r = self.ring(es, "utg", [128, 512], F32, 2)
        tu_r = self.ring(es, "utu", [128, 512], F32, 2)
        o_r = self.ring(es, "uo", [128, 512], BF16, 3)
        for ti, (t0, w) in enumerate(TILES):
            p.dma("sp" if ti % 2 == 0 else "act", HT[:, :, t0:t0 + w], self.tv(self.hT, ti, KC, w), w=[HT.k(ti)])
        it = [0]

        def epilogue(j, E, ti):
            t0, w = TILES[ti]
            outs = []
            for gi, (Eb, ring, pc) in enumerate(((E[0], tg_r, j), (E[1], tu_r, 43 + j))):
                Tt = ring.get()
                wc = lambda tap, pc=pc: self.prm(l, "fconv", pc * 3 + tap)
                self.act(Tt[:, 0:w], Eb[:, 1:1 + w], AF.Identity, [Eb.k(), self.PRMb.k()], [Tt.k()], scale=wc(1), bias=self.prm(l, "fcb", pc))
                self.stt(Tt[:, 0:w], Eb[:, 0:w], wc(0), Tt[:, 0:w], ALU.mult, ALU.add, [Eb.k(), Tt.k(), self.PRMb.k()], [Tt.k()])
                self.stt(Tt[:, 0:w], Eb[:, 2:2 + w], wc(2), Tt[:, 0:w], ALU.mult, ALU.add, [Eb.k(), Tt.k(), self.PRMb.k()], [Tt.k()])
                outs.append(Tt)
            TG, TU = outs
            self.act(TG[:, 0:w], TG[:, 0:w], AF.Silu, [TG.k()], [TG.k()])
            O = o_r.get()
            self.tt("dve", O[:, 0:w], TG[:, 0:w], TU[:, 0:w], ALU.mult, [TG.k(), TU.k()], [O.k()])
            p.dma("sp", self.actT[ti][:, j * 512:j * 512 + w], O[:, 0:w], r=[O.k()], w=[self.actT.k(j, ti)])

        for j in range(43):
            Ws = []
            for pc in (j, 43 + j):
                W = w_ring.get()
                p.dma("pool", W[:, :, :], self.w_up[l][:, pc * 128:(pc + 1) * 128].rearrange("(k p) n -> p k n", p=128), w=[W.k()])
                Ws.append(W)
            prev = None
            for ti, (t0, w) in enumerate(TILES):
                E = (eg_r.get(), eu_r.get())
                for gi in range(2):
                    P = self.PS[it[0] % 4]
                    it[0] += 1
                    W = Ws[gi]
                    fns = [(lambda e, k=k, W=W, P=P, t0=t0, w=w: e.matmul(P[:, 0:w], W[:, k, :], HT[:, k, t0:t0 + w],
                                                                             start=(k == 0), stop=(k == KC - 1))) for k in range(KC)]
                    p.mm(fns, r=[W.k(), HT.k(ti)], w=[P.k()])
                    Eb = E[gi]
                    self.cp("act" if gi == 0 else "dve", Eb[:, 1:1 + w], P[:, 0:w], [P.k()], [Eb.k()])
                    if ti == 0 or ti == 8:
                        p.op("pool", lambda e, Eb=Eb: e.memset(Eb[:, 0:1], 0.0), w=[Eb.k()])
                    else:
                        pw = TILES[prev[1]][1]
                        self.cp("pool", Eb[:, 0:1], prev[0][gi][:, pw:pw + 1], [prev[0][gi].k()], [Eb.k()])
                    if prev is not None:
                        pw = TILES[prev[1]][1]
                        pE = prev[0][gi]
                        if ti == 8:
                            p.op("pool", lambda e, pE=pE, pw=pw: e.memset(pE[:, pw + 1:pw + 2], 0.0), w=[pE.k()])
                        else:
                            self.cp("pool", pE[:, pw + 1:pw + 2], Eb[:, 1:2], [Eb.k()], [pE.k()])
                if prev is not None:
                    epilogue(j, prev[0], prev[1])
                prev = (E, ti)
            for gi in range(2):
                pE = prev[0][gi]
                pw = TILES[prev[1]][1]
                p.op("pool", lambda e, pE=pE, pw=pw: e.memset(pE[:, pw + 1:pw + 2], 0.0), w=[pE.k()])
            epilogue(j, prev[0], prev[1])

    def st_final(self):
        p = self.p
        es = self.stage_es
        if True:
            xr = self.ring(es, "fx", [128, KC, 512], F32, 2)
            sq = self.sb(es, "fsq", [128, KC, 512], F32)
            ss = self.sb(es, "fss", [128, 512], F32)
            rs = self.sb(es, "frs", [128, 512], F32)
            o = self.poff["fng"]
            for ti, (t0, w) in enumerate(TILES[:8]):
                X = xr.get()
                p.dma("sp", X[:, :, :], self.xs[:, t0:t0 + w].rearrange("(k p) t -> p k t", p=128), w=[X.k()])
                self.act(sq[:, :, :], X[:, :, :], AF.Square, [X.k()], [sq.k()])
                p.op("dve", lambda e: e.tensor_reduce(out=ss[:, :], in_=sq[:, :, :].rearrange("p k t -> p t k"), axis=AX.X, op=ALU.add),
                     [sq.k()], [ss.k()])
                P = self.PS[ti % 2]
                self.mm1(P[:, :], self.C("ones"), ss[:, :], [ss.k(), self.cstb.k()], [P.k()])
                self.act(rs[:, :], P[:, :], AF.Sqrt, [P.k(), self.epsb.k()], [rs.k()], scale=1.0 / D, bias=self.epsb[:, 0:1])
                p.op("dve", lambda e: e.reciprocal(out=rs[:, :], in_=rs[:, :]), [rs.k()], [rs.k()])
                self.tt("dve", sq[:, :, :], X[:, :, :], rs[:, :].unsqueeze(1).to_broadcast([128, KC, 512]), ALU.mult, [X.k(), rs.k()], [sq.k()])
                for k in range(KC):
                    self.ts("pool" if k % 2 else "dve", sq[:, k, :], sq[:, k, :], self.PRM[:, o + k:o + k + 1], None, ALU.mult, None,
                            [sq.k(), self.PRMb.k()], [sq.k()])
                p.dma("sp", self.y_out[:, t0:t0 + w].rearrange("(k p) t -> p k t", p=128), sq[:, :, :], r=[sq.k()], w=[self.y_out.k(ti)])


def make_in_maps(I, n_cores, batches):
    cst, _ = _consts()
    prm = _pack_params(I)
    lgw = _pack_lgw(I)
    rope = _rope_tables()
    f = lambda a: np.ascontiguousarray(np.asarray(a, dtype=np.float32))
    shared = dict(ada_w=f(I["ada_w"]), w_in=f(I["w_in"]), w_out=f(I["w_out"]), ffn_w_up=f(I["ffn_w_up"]),
                  ffn_w_down=f(I["ffn_w_down"]), prm=prm, lgw=lgw, cst=cst, rope=rope)
    maps = []
    for b in batches:
        xT = np.ascontiguousarray(np.concatenate([I["x"][b].T, I["ctx"][b].T], axis=1))
        cv = np.stack([I["c"][b], I["c_ctx"]], axis=0)
        cT = np.ascontiguousarray(cv.reshape(2, KC, 128).transpose(2, 1, 0))
        m = dict(shared)
        m["xT"] = xT
        m["cT"] = cT
        maps.append(m)
    return maps


_CACHE = {}


def kernel(**inputs):
    I = {k: np.asarray(v) for k, v in inputs.items()}
    B = I["x"].shape[0]
    if "prog" not in _CACHE:
        _CACHE["prog"] = Builder().build()
    bld = _CACHE["prog"]
    maps = make_in_maps(I, B, list(range(B)))
    res = run_bass_kernel_spmd(bld.nc, maps, core_ids=list(range(B)))
    out = np.stack([np.ascontiguousarray(res.results[b]["yT"].T) for b in range(B)], axis=0)
    return out.astype(np.float32)
```
